# Optimizing a Trainium2 kernel written in Bass

```python
import math
import jax, jax.numpy as jnp
from jax import lax
import numpy as np

D_MODEL = 1024
BATCH = 8
SEQ = 4096
DEPTH = 2

N_A_LAYERS = DEPTH // 2
N_B_LAYERS = DEPTH - N_A_LAYERS

HEAD_DIM = 128
GDN_HEADS = D_MODEL // HEAD_DIM
GDN_DK = HEAD_DIM
GDN_DV = HEAD_DIM
GDN_WIDTH = GDN_HEADS * HEAD_DIM
CONV_K = 4
CHUNK = 64
FOX_HEADS = D_MODEL // HEAD_DIM
FOX_WIDTH = FOX_HEADS * HEAD_DIM
Q_BLOCK = 128
EPS = 1e-6

kernel_name = "yoco_gdn_fox_hybrid"


def rms_norm(x, g):
    xf = x.astype(jnp.float32)
    y = xf * lax.rsqrt(jnp.mean(xf * xf, axis=-1, keepdims=True) + EPS)
    return (y * g.astype(jnp.float32)).astype(x.dtype)


def l2_norm(x):
    xf = x.astype(jnp.float32)
    return (xf * lax.rsqrt(jnp.sum(xf * xf, axis=-1, keepdims=True) + EPS)).astype(x.dtype)


def causal_depthwise_conv(x, w):
    rhs = w[:, None, :].astype(x.dtype)
    return lax.conv_general_dilated(
        x, rhs, window_strides=(1,), padding=((CONV_K - 1, 0),),
        dimension_numbers=("NWC", "WIO", "NWC"), feature_group_count=x.shape[-1])


def chunk_gated_delta_rule(q, k, v, g, beta):
    B, L, H, Dk = q.shape
    Dv = v.shape[-1]
    N = L // CHUNK
    f32 = jnp.float32
    out_dtype = v.dtype

    def chunks(t):
        t = jnp.moveaxis(t.astype(f32), 2, 1)
        return t.reshape((B, H, N, CHUNK) + t.shape[3:])

    qc = chunks(q) * (Dk ** -0.5)
    kc, vc = chunks(k), chunks(v)
    gc, bc = chunks(g), chunks(beta)
    gcum = jnp.cumsum(gc, axis=-1)
    causal = jnp.tril(jnp.ones((CHUNK, CHUNK), dtype=bool))
    strict = jnp.tril(jnp.ones((CHUNK, CHUNK), dtype=bool), -1)
    diff = gcum[..., :, None] - gcum[..., None, :]
    decay = jnp.where(causal, jnp.exp(jnp.where(causal, diff, 0.0)), 0.0)

    kb = kc * bc[..., None]
    a_mat = jnp.where(strict, jnp.einsum('bhncd,bhnsd->bhncs', kb, kc) * decay, 0.0)
    rhs = jnp.concatenate([vc * bc[..., None], kb * jnp.exp(gcum)[..., None]], axis=-1)
    sol = lax.linalg.triangular_solve(a_mat, rhs, left_side=True, lower=True,
                                      unit_diagonal=True)
    u, w = sol[..., :Dv], sol[..., Dv:]
    attn = jnp.einsum('bhncd,bhnsd->bhncs', qc, kc) * decay

    def step(S, inp):
        q_i, k_i, u_i, w_i, g_i, attn_i = inp
        v_new = u_i - jnp.einsum('bhck,bhkv->bhcv', w_i, S)
        o = (jnp.einsum('bhck,bhkv->bhcv', q_i * jnp.exp(g_i)[..., None], S)
             + jnp.einsum('bhcs,bhsv->bhcv', attn_i, v_new))
        g_last = g_i[..., -1]
        k_dec = k_i * jnp.exp(g_last[..., None] - g_i)[..., None]
        S = S * jnp.exp(g_last)[..., None, None] + jnp.einsum('bhck,bhcv->bhkv', k_dec, v_new)
        return S, o

    xs = tuple(jnp.moveaxis(t, 2, 0) for t in (qc, kc, u, w, gcum, attn))
    S0 = jnp.zeros((B, H, Dk, Dv), f32)
    _, o = lax.scan(step, S0, xs)
    o = o.transpose(1, 0, 3, 2, 4).reshape(B, L, H, Dv)
    return o.astype(out_dtype)


def gdn_mixer(h, w_in, conv_w, a_log, dt_bias, o_norm, w_out):
    B, L, _ = h.shape
    W, H = GDN_WIDTH, GDN_HEADS
    proj = h @ w_in
    qkv, z, a, b = jnp.split(proj, [3 * W, 4 * W, 4 * W + H], axis=-1)
    qkv = jax.nn.silu(causal_depthwise_conv(qkv, conv_w))
    q, k, v = jnp.split(qkv, 3, axis=-1)
    q = l2_norm(q.reshape(B, L, H, GDN_DK))
    k = l2_norm(k.reshape(B, L, H, GDN_DK))
    v = v.reshape(B, L, H, GDN_DV)
    beta = jax.nn.sigmoid(b.astype(jnp.float32))
    g = -jnp.exp(a_log.astype(jnp.float32)) * jax.nn.softplus(a.astype(jnp.float32) + dt_bias.astype(jnp.float32))
    o = chunk_gated_delta_rule(q, k, v, g, beta)
    o = rms_norm(o, o_norm) * jax.nn.silu(z.reshape(B, L, H, GDN_DV))
    return o.reshape(B, L, W) @ w_out


def shared_kv(h, kv_norm, kv_w, kv_forget_bias, kv_k_norm):
    B, L, _ = h.shape
    u = rms_norm(h, kv_norm)
    k, v, f = jnp.split(u @ kv_w, [FOX_WIDTH, 2 * FOX_WIDTH], axis=-1)
    k = rms_norm(k.reshape(B, L, FOX_HEADS, HEAD_DIM), kv_k_norm)
    v = v.reshape(B, L, FOX_HEADS, HEAD_DIM)
    log_f = jax.nn.log_sigmoid(f.astype(jnp.float32) + kv_forget_bias.astype(jnp.float32))
    c = jnp.cumsum(log_f, axis=1)
    return k, v, c


def fox_attention(q, k, v, c):
    B, L, H, D = q.shape
    nb = L // Q_BLOCK
    qb = q.reshape(B, nb, Q_BLOCK, H, D).transpose(1, 0, 2, 3, 4)
    cT = jnp.transpose(c, (0, 2, 1))
    cb = cT.reshape(B, H, nb, Q_BLOCK).transpose(2, 0, 1, 3)
    kpos = jnp.arange(L)
    scale = D ** -0.5

    def block(args):
        i, q_i, c_i = args
        s = jnp.einsum('bqhd,bkhd->bhqk', q_i, k, preferred_element_type=jnp.float32) * scale
        s = s + (c_i[..., :, None] - cT[..., None, :])
        qpos = i * Q_BLOCK + jnp.arange(Q_BLOCK)
        s = jnp.where(kpos[None, :] <= qpos[:, None], s, -jnp.inf)
        p = jax.nn.softmax(s, axis=-1)
        return jnp.einsum('bhqk,bkhd->bqhd', p.astype(v.dtype), v)

    o = lax.map(block, (jnp.arange(nb), qb, cb))
    return o.transpose(1, 0, 2, 3, 4).reshape(B, L, H, D)


def fox_mixer(h, k, v, c, w_in, q_norm, w_out):
    B, L, _ = h.shape
    q, z = jnp.split(h @ w_in, [FOX_WIDTH], axis=-1)
    q = rms_norm(q.reshape(B, L, FOX_HEADS, HEAD_DIM), q_norm)
    o = fox_attention(q, k, v, c)
    o = o * jax.nn.silu(z.reshape(B, L, FOX_HEADS, HEAD_DIM))
    return o.reshape(B, L, FOX_WIDTH) @ w_out


def setup_inputs(seed: int = 0) -> dict:
    key = jax.random.key(seed)
    ks = jax.random.split(key, 24)
    D, W, H = D_MODEL, GDN_WIDTH, GDN_HEADS
    nA, nB = N_A_LAYERS, N_B_LAYERS

    def nrm(k, shape, fan_in):
        return jax.random.normal(k, shape, jnp.float32) * (fan_in ** -0.5)

    def gain(k, shape):
        return 1.0 + 0.05 * jax.random.normal(k, shape, jnp.float32)

    dt = jnp.exp(jax.random.uniform(ks[5], (nA, H), jnp.float32, math.log(1e-3), math.log(1e-1)))
    return {
        "x": jax.random.normal(ks[0], (BATCH, SEQ, D), jnp.float32),
        "gdn_pre_norm": gain(ks[1], (nA, D)),
        "gdn_w_in": nrm(ks[2], (nA, D, 4 * W + 2 * H), D),
        "gdn_conv_w": nrm(ks[3], (nA, CONV_K, 3 * W), CONV_K),
        "gdn_a_log": jnp.log(jax.random.uniform(ks[4], (nA, H), jnp.float32, 1.0, 16.0)),
        "gdn_dt_bias": dt + jnp.log(-jnp.expm1(-dt)),
        "gdn_o_norm": gain(ks[6], (nA, GDN_DV)),
        "gdn_w_out": nrm(ks[7], (nA, W, D), W),
        "gdn_post_norm": gain(ks[8], (nA, D)),
        "kv_norm": gain(ks[9], (D,)),
        "kv_w": nrm(ks[10], (D, 2 * FOX_WIDTH + FOX_HEADS), D),
        "kv_forget_bias": 1.0 + 0.5 * jax.random.normal(ks[11], (FOX_HEADS,), jnp.float32),
        "kv_k_norm": gain(ks[12], (HEAD_DIM,)),
        "fox_pre_norm": gain(ks[13], (nB, D)),
        "fox_w_in": nrm(ks[14], (nB, D, 2 * FOX_WIDTH), D),
        "fox_q_norm": gain(ks[15], (nB, HEAD_DIM)),
        "fox_w_out": nrm(ks[16], (nB, FOX_WIDTH, D), FOX_WIDTH),
        "fox_post_norm": gain(ks[17], (nB, D)),
    }


def reference(x, gdn_pre_norm, gdn_w_in, gdn_conv_w, gdn_a_log, gdn_dt_bias, gdn_o_norm,
              gdn_w_out, gdn_post_norm, kv_norm, kv_w, kv_forget_bias, kv_k_norm,
              fox_pre_norm, fox_w_in, fox_q_norm, fox_w_out, fox_post_norm):
    h = x
    k_sh = v_sh = c_sh = None
    for layer in range(DEPTH):
        if layer < N_A_LAYERS:
            i = layer
            y = gdn_mixer(rms_norm(h, gdn_pre_norm[i]), gdn_w_in[i], gdn_conv_w[i],
                          gdn_a_log[i], gdn_dt_bias[i], gdn_o_norm[i], gdn_w_out[i])
            h = h + rms_norm(y, gdn_post_norm[i])
        else:
            if layer == N_A_LAYERS:
                k_sh, v_sh, c_sh = shared_kv(h, kv_norm, kv_w, kv_forget_bias, kv_k_norm)
            i = layer - N_A_LAYERS
            y = fox_mixer(rms_norm(h, fox_pre_norm[i]), k_sh, v_sh, c_sh,
                          fox_w_in[i], fox_q_norm[i], fox_w_out[i])
            h = h + rms_norm(y, fox_post_norm[i])
    return h
```

```python
import contextlib
import numpy as np
import concourse.bass as bass
import concourse.mybir as mybir
from concourse.bass_utils import run_bass_kernel_spmd

F32 = mybir.dt.float32
BF16 = mybir.dt.bfloat16
AF = mybir.ActivationFunctionType
ALU = mybir.AluOpType

D = 1024
H = 8
EPS = 1e-6
NEG = -30000.0
ENGS = ("pe", "act", "dve", "pool", "sp")

SYNC_ALL = False
BAR_DMA = True


class Op:
    __slots__ = ("eng", "fn", "reads", "writes", "is_dma", "chan", "waits", "all_dma")


class Prog:
    def __init__(self, nc):
        self.nc = nc
        self.ops = []
        self.stack = contextlib.ExitStack()

    def sbuf(self, name, shape, dtype):
        return self.stack.enter_context(self.nc.sbuf_tensor("sb_" + name, list(shape), dtype))

    def psum(self, name, shape, dtype):
        return self.stack.enter_context(self.nc.psum_tensor(name, list(shape), dtype))

    def op(self, eng, fn, reads=(), writes=()):
        o = Op()
        o.eng, o.fn, o.reads, o.writes = eng, fn, tuple(reads), tuple(writes)
        o.is_dma, o.chan, o.all_dma = False, None, False
        self.ops.append(o)
        return o

    def dma(self, eng, fn, chan, reads=(), writes=()):
        o = self.op(eng, fn, reads, writes)
        o.is_dma, o.chan = True, chan
        return o

    def finalize(self, final_wait_tokens=()):
        nc = self.nc
        ops = self.ops
        last_w, readers = {}, {}
        eng_know = {e: {} for e in ENGS}
        cnt = {}
        op_sig = [None] * len(ops)
        op_know = [None] * len(ops)
        waited_on = set()
        last_dma = {}
        for i, o in enumerate(ops):
            deps = set()
            if o.all_dma:
                deps.update(last_dma.values())
            if o.is_dma:
                last_dma[o.chan] = i
            for t in o.reads:
                w = last_w.get(t)
                if w is not None:
                    deps.add(w)
            for t in o.writes:
                w = last_w.get(t)
                if w is not None:
                    deps.add(w)
                for r in readers.get(t, ()):
                    deps.add(r)
            know = eng_know[o.eng]
            best = {}
            for d in deps:
                if d == i:
                    continue
                do = ops[d]
                sk, ordv = op_sig[d]
                if (not do.is_dma) and do.eng == o.eng and not o.is_dma:
                    if o.eng in ("pe", "sp"):
                        continue
                    if not SYNC_ALL and not any(t in do.writes for t in o.reads):
                        continue
                if know.get(sk, 0) >= ordv:
                    continue
                if sk not in best or op_sig[best[sk]][1] < ordv:
                    best[sk] = d
            o.waits = []
            for sk, d in best.items():
                if know.get(sk, 0) >= op_sig[d][1]:
                    continue
                o.waits.append(d)
                waited_on.add(d)
                for k2, v2 in op_know[d].items():
                    if know.get(k2, 0) < v2:
                        know[k2] = v2
                know[sk] = max(know.get(sk, 0), op_sig[d][1])
            sk = ("c", o.chan) if o.is_dma else ("e", o.eng)
            cnt[sk] = cnt.get(sk, 0) + 1
            op_sig[i] = (sk, cnt[sk])
            op_know[i] = dict(know)
            for t in o.reads:
                readers.setdefault(t, []).append(i)
            for t in o.writes:
                last_w[t] = i
                readers[t] = []
        final_deps = set()
        for t in final_wait_tokens:
            w = last_w.get(t)
            if w is not None:
                final_deps.add(w)
                waited_on.add(w)
        inc_val, running = {}, {}
        for i, o in enumerate(ops):
            sk = op_sig[i][0]
            if o.is_dma:
                running[sk] = running.get(sk, 0) + 16
                inc_val[i] = running[sk]
            elif i in waited_on:
                running[sk] = running.get(sk, 0) + 1
                inc_val[i] = running[sk]
        sems = {}
        for sk in running:
            sems[sk] = self.stack.enter_context(nc.semaphore(f"sem{len(sems)}"))
        per_eng = {e: [] for e in ENGS}
        for i, o in enumerate(ops):
            per_eng[o.eng].append(i)
        self.stats = dict(n_ops=len(ops), n_waits=sum(len(o.waits) for o in ops), n_sems=len(sems),
                          per_eng={e: len(v) for e, v in per_eng.items()})

        def emit_engine(ename, eobj, extra_final=False):
            for i in per_eng[ename]:
                o = ops[i]
                for d in o.waits:
                    eobj.wait_ge(sems[op_sig[d][0]], inc_val[d])
                ins = o.fn(eobj)
                if i in inc_val:
                    ins.then_inc(sems[op_sig[i][0]], 16 if o.is_dma else 1)
            if extra_final:
                for d in sorted(final_deps):
                    eobj.wait_ge(sems[op_sig[d][0]], inc_val[d])

        with nc.Block() as block:
            @block.sync
            def _(e):
                emit_engine("sp", e, extra_final=True)

            @block.tensor
            def _(e):
                emit_engine("pe", e)

            @block.scalar
            def _(e):
                emit_engine("act", e)

            @block.vector
            def _(e):
                emit_engine("dve", e)

            @block.gpsimd
            def _(e):
                emit_engine("pool", e)
        self.stack.close()


C_ID, C_LTRI, C_SELOWN, C_SEL0, C_SEL1, C_MASKI, C_MASKS, C_LFULL, C_ONES, C_MTRI = range(10)
NCONST = 10


def make_consts():
    p = np.arange(128)[:, None]
    i = np.arange(128)[None, :]
    same = (p // 64) == (i // 64)
    c = np.zeros((NCONST, 128, 128), np.float32)
    c[C_ID] = (p == i)
    c[C_LTRI] = (p <= i) & same
    c[C_SELOWN] = (p == 64 * (i // 64) + 63)
    c[C_SEL0] = (p == 63) & (i >= 0)
    c[C_SEL1] = (p == 127) & (i >= 0)
    c[C_MASKI] = np.where((i >= p) & same, 0.0, NEG)
    c[C_MASKS] = np.where((i > p) & same, 0.0, NEG)
    c[C_LFULL] = (p <= i)
    c[C_ONES] = 1.0
    c[C_MTRI] = np.where(i >= p, 0.0, NEG)
    return c


class Builder:
    def __init__(self, L, mode="both", dbg=()):
        self.L = L
        self.mode = mode
        self.dbg = set(dbg)
        self.NB = L // 128
        self.TG = min(512, L)
        self.NG = L // self.TG
        self.NBG = self.TG // 128
        self.nc = bass.Bass("TRN2", target_bir_lowering=False)
        self.P = Prog(self.nc)
        self.final_tokens = []
        self.dram = {}
        self.prep = []
        self.prep_done = 0
        self.prep_issued = 0

    def din(self, name, shape, dt=F32):
        t = self.nc.dram_tensor(name, list(shape), dt, kind="ExternalInput").ap()
        self.dram[name] = t
        return t

    def dout(self, name, shape, dt=F32):
        t = self.nc.dram_tensor(name, list(shape), dt, kind="ExternalOutput").ap()
        self.dram[name] = t
        return t

    def dscr(self, name, shape, dt=F32):
        return self.nc.dram_tensor(name, list(shape), dt, kind="Internal").ap()

    def act(self, out, in_, func, r, w, **kw):
        self.P.op("act", lambda e: e.activation(out=out, in_=in_, func=func, **kw), r, w)

    def tt(self, eng, out, in0, in1, op, r, w):
        self.P.op(eng, lambda e: e.tensor_tensor(out=out, in0=in0, in1=in1, op=op), r, w)

    def ts(self, eng, out, in0, s1, op0, r, w, s2=None, op1=None):
        if op1 is None and eng == "pool" and op0 == ALU.mult:
            s2, op1 = 1.0, ALU.mult
        if op1 is None:
            self.P.op(eng, lambda e: e.tensor_scalar(out=out, in0=in0, scalar1=s1, scalar2=None, op0=op0), r, w)
        else:
            self.P.op(eng, lambda e: e.tensor_scalar(out=out, in0=in0, scalar1=s1, scalar2=s2, op0=op0, op1=op1), r, w)

    def stt(self, eng, out, in0, scalar, in1, op0, op1, r, w):
        self.P.op(eng, lambda e: e.scalar_tensor_tensor(out=out, in0=in0, scalar=scalar, in1=in1, op0=op0, op1=op1), r, w)

    def cp(self, eng, out, in_, r, w):
        if eng == "act":
            self.P.op(eng, lambda e: e.activation(out=out, in_=in_, func=AF.Copy), r, w)
        else:
            self.P.op(eng, lambda e: e.tensor_copy(out=out, in_=in_), r, w)

    def mm(self, out, lhsT, rhs, start, stop, r, w):
        self.P.op("pe", lambda e: e.matmul(out=out, lhsT=lhsT, rhs=rhs, start=start, stop=stop), r, w)

    def tr(self, out, in_, r, w):
        idb = self.idb
        self.P.op("pe", lambda e: e.transpose(out=out, in_=in_, identity=idb[:]), tuple(r) + ("idb",), w)

    def dma(self, out, in_, chan, r, w, eng="sp"):
        self.P.dma(eng, lambda e: e.dma_start(out=out, in_=in_), chan, r, w)

    def memset(self, eng, ap, val, w):
        self.P.op(eng, lambda e: e.memset(ap, val), (), w)

    def bank(self, b, n=1):
        return self.PS[:, b * 512:(b + n) * 512]

    def bank_bf(self, b):
        return self.PS[:, b * 512:(b + 1) * 512].bitcast(BF16)

    def debug_out(self, name, sb_ap, shape, token, dt=F32):
        if name not in self.dbg:
            return
        d = self.dout("dbg_" + name, shape, dt)
        self.dma(d, sb_ap, "dbg_" + name, [token], ["dbgo_" + name])
        self.final_tokens.append("dbgo_" + name)

    def arena_reset(self):
        self.a_off = 0

    def al(self, name, shape, dtype):
        shape = list(shape)
        n = 1
        for d_ in shape[1:]:
            n *= d_
        words = n if dtype == F32 else (n + 1) // 2
        words = (words + 1) // 2 * 2
        assert self.a_off + words <= self.ARENA_W, (name, self.a_off, words)
        ap = self.arena[0:shape[0], self.a_off:self.a_off + words]
        self.a_off += words
        if dtype != F32:
            ap = ap.bitcast(dtype)
        ap = ap[:, 0:n]
        if len(shape) == 3:
            ap = ap.rearrange("p (a b) -> p a b", a=shape[1])
        elif len(shape) == 4:
            ap = ap.rearrange("p (a b c) -> p a b c", a=shape[1], b=shape[2])
        return ap

    def join(self, tokens, col):
        jn = self.jn
        self.P.op("pool", lambda e: e.memset(jn[0:1, col:col + 1], 0.0), tuple(tokens), tuple(tokens))

    def prep_add(self, fn_in, fn_rest):
        self.prep.append((fn_in, fn_rest))

    def prep_emit(self, n):
        for _ in range(n):
            p = self.prep_done
            if p >= len(self.prep):
                return
            if self.prep_issued <= p:
                self.prep[p][0](p % 2)
                self.prep_issued = p + 1
            if p + 1 < len(self.prep) and self.prep_issued <= p + 1:
                self.prep[p + 1][0]((p + 1) % 2)
                self.prep_issued = p + 2
            self.prep[p][1](p % 2)
            self.prep_done += 1

    def prep_emit_until(self, idx):
        while self.prep_done <= idx and self.prep_done < len(self.prep):
            self.prep_emit(1)

    def barrier(self, tag):
        bar = self.bar
        engs = ["act", "dve", "pool", "pe", "sp"]
        for i, e_ in enumerate(engs):
            n0 = len(self.P.ops)
            self._dummy(e_, i, [], [f"bar_{tag}_{e_}"])
            self.P.ops[n0].all_dma = BAR_DMA
        for i, e_ in enumerate(engs):
            self._dummy(e_, i, [f"bar_{tag}_{x}" for x in engs], [f"bar2_{tag}_{e_}"])

    def _dummy(self, eng, i, r, w):
        bar = self.bar
        if eng == "act":
            self.P.op("act", lambda e: e.activation(out=bar[0:1, i:i + 1], in_=bar[0:1, 8:9], func=AF.Copy), r, w)
        elif eng in ("dve", "pool"):
            self.P.op(eng, lambda e: e.memset(bar[0:1, i:i + 1], 0.0), r, w)
        elif eng == "pe":
            PS = self.PS
            self.P.op("pe", lambda e: e.matmul(out=PS[0:1, 4095:4096], lhsT=self.idb[:, 0:1], rhs=self.idb[:, 0:1], start=True, stop=True),
                      tuple(r) + ("idb",), tuple(w) + ("ps7",))
        else:
            self.P.dma("sp", lambda e: e.dma_start(out=self.bar_d[0:1, 0:8], in_=bar[0:1, 8:16]), "bar", r, w)

    def setup_common(self):
        P = self.P
        self.consts_d = self.din("consts", [NCONST, 128, 128])
        self.cst = P.sbuf("cst", [128, NCONST, 128], F32)
        self.idb = P.sbuf("idb", [128, 128], BF16)
        self.onesb = P.sbuf("onesb", [128, 128], BF16)
        self.PS = P.psum("PS", [128, 4096], F32)
        self.bar = P.sbuf("bar", [1, 16], F32)
        self.jn = P.sbuf("jn", [1, 16], F32)
        self.bar_d = self.dscr("bar_d", [1, 8])
        self.memset("pool", self.bar[:], 0.0, ["bar_init"])
        self.stg = [P.sbuf(f"stg{i}", [128, 1040], F32) for i in range(2)]
        self.ARENA_W = 48600
        self.arena = P.sbuf("arena", [128, self.ARENA_W], F32)
        self.arena_reset()
        self.dma(self.cst[:], self.consts_d.rearrange("n p f -> p n f"), "cst", [], ["cst"])
        self.cp("pool", self.idb[:], self.cst[:, C_ID, :], ["cst"], ["idb"])
        self.cp("pool", self.onesb[:], self.cst[:, C_ONES, :], ["cst"], ["onesb"])

    def layer1(self, x_d, h1_d):
        P = self.P
        L, TG, NG, NBG = self.L, self.TG, self.NG, self.NBG
        cst = self.cst
        idb = self.idb
        w1qkv_d = self.din("w1qkv", [24, 128, 8, 128])
        w1z_d = self.din("w1z", [128, 8, 1040])
        gpre_d = self.din("gpre", [128, 8])
        cw_d = self.din("cw", [128, 24, 4])
        alog_d = self.din("alog_b", [128, 8])
        dtb_d = self.din("dtb_b", [128, 8])
        gon_d = self.din("gonorm", [128, 1])
        wout_d = self.din("wout1", [128, 8, 1024])
        gpost_d = self.din("gpost1", [1, 1024])
        W1s = self.dscr("W1s", [24, 128, 1024], BF16)
        gcs = self.dscr("gcs", [NG, NBG * 8, 128], F32)
        gpre = self.al("gpre", [128, 8], F32)
        cw = self.al("cw", [128, 24, 4], F32)
        convd = self.al("convd", [128, 24, 4, 128], BF16)
        negA = self.al("negA", [128, 8], F32)
        dtb = self.al("dtb", [128, 8], F32)
        gon = self.al("gon", [128, 1], F32)
        gpost = self.al("gpost", [128, 1024], F32)
        Wout = self.al("Wout", [128, 8, 1024], BF16)
        W1z = self.al("W1z", [128, 8, 1040], BF16)
        stg = self.stg
        wbuf = [self.al(f"wbuf{i}", [128, 8, 128], BF16) for i in range(3)]
        xt = [self.al(f"xt{i}", [128, 1024], F32) for i in range(2)]
        junk = self.al("junk", [128, 1024], BF16)
        xn2 = [self.al(f"xn{i}", [128, 1024], BF16) for i in range(2)]
        xnT = self.al("xnT", [128, 8, TG], BF16)
        sm = self.al("sm", [128, 64], F32)
        pc = [self.al(f"pc{i}", [128, TG + 3], BF16) for i in range(3)]
        halo = self.al("halo", [128, 24, 3], BF16)
        qkvT = self.al("qkvT", [128, 24, TG], BF16)
        sq = [self.al(f"sq{i}", [128, TG], BF16) for i in range(2)]
        zs = self.al("zs", [128, NBG, 1024], BF16)
        ab = self.al("ab", [128, NBG, 16], F32)
        def sc(name, n=8):
            return self.al(name, [128, NBG, n], F32)
        g_t, lnb, beta, gc, glown = sc("g_t"), sc("lnb"), sc("beta"), sc("gc"), sc("glown")
        lnss = sc("lnss", 16)
        rq, rk, nk, lnrk = sc("rq"), sc("rk"), sc("nk"), sc("lnrk")
        sc_kg, sc_kd, sc_vnew, sc_o, sc_oi, bias1, bias2 = (sc("sc_kg"), sc("sc_kd"), sc("sc_vnew"), sc("sc_o"),
                                                              sc("sc_oi"), sc("bias1"), sc("bias2"))
        tmp8 = sc("tmp8")
        dec = self.al("dec", [128, NBG, 2, 8], F32)
        gcT = self.al("gcT", [NBG * 8, 128], F32)
        kg = self.al("kg", [128, 8, 128], BF16)
        kd = self.al("kd", [128, 8, 128], BF16)
        vn = self.al("vn", [128, 8, 128], BF16)
        attnT = self.al("attnT", [128, 8, 128], BF16)
        nwT = self.al("nwT", [128, 8, 128], BF16)
        vnew = self.al("vnew", [128, 8, 128], BF16)
        X1 = self.al("X1", [128, 8, 128], F32)
        X2 = self.al("X2", [128, 8, 128], F32)
        Pm = [self.al(f"Pm{i}", [128, 8, 128], BF16) for i in range(2)]
        Qm = [self.al(f"Qm{i}", [128, 8, 128], BF16) for i in range(2)]
        TTm = [self.al(f"TTm{i}", [128, 8, 128], BF16) for i in range(2)]
        S32 = self.al("S32", [128, 8, 128], F32)
        Sd = self.al("Sd", [128, 8, 128], F32)
        Sbf = self.al("Sbf", [128, 8, 128], BF16)
        t1 = self.al("t1", [128, 8, 128], F32)
        t2 = self.al("t2", [128, 8, 128], F32)
        og = self.al("og", [128, 8, 128], BF16)
        ogT = self.al("ogT", [128, 8, 128], BF16)
        sso = self.al("sso", [128, 8], F32)
        so = self.al("so", [128, 8], F32)
        xr = self.al("xr", [128, 1024], F32)

        def bc8(ap2):
            return ap2.unsqueeze(2).to_broadcast([128, 8, 128])

        self.dma(gpre[:], gpre_d, "iniA", [], ["gpre"])
        self.dma(cw[:], cw_d, "iniA", [], ["cw"])
        self.dma(negA[:], alog_d, "iniA", [], ["negA"])
        self.dma(dtb[:], dtb_d, "iniA", [], ["dtb"])
        self.dma(gon[:], gon_d, "iniA", [], ["gon"])
        self.dma(gpost[:], gpost_d.partition_broadcast(128), "iniA", [], ["gpost"])
        self.join(["gpre", "cw", "negA", "dtb", "gon", "gpost"], 9)
        self.act(negA[:], negA[:], AF.Exp, ["negA"], ["negA"])
        self.ts("dve", negA[:], negA[:], -1.0, ALU.mult, ["negA"], ["negA"])
        for k in range(4):
            self.tt("pool", convd[:, :, k, :], idb[:].unsqueeze(1).to_broadcast([128, 24, 128]),
                    cw[:, :, k].unsqueeze(2).to_broadcast([128, 24, 128]), ALU.mult,
                    ["idb", "cw"], ["convd"])
        for c in range(24):
            def fin(slot, c=c):
                sv = stg[slot][:, 0:1024].rearrange("p (k n) -> p k n", k=8)
                self.dma(sv, w1qkv_d[c], f"stg{slot}", [], [f"stg{slot}"], eng="pool")

            def frest(slot, c=c):
                sv = stg[slot][:, 0:1024].rearrange("p (k n) -> p k n", k=8)
                wb = wbuf[c % 3]
                self.tt("pool", wb[:], sv, gpre[:, :].unsqueeze(2).to_broadcast([128, 8, 128]), ALU.mult,
                        [f"stg{slot}", "gpre"], [f"wbuf{c % 3}"])
                self.dma(W1s[c].rearrange("p (k n) -> p k n", k=8), wb[:], f"w1s{c % 3}", [f"wbuf{c % 3}"], [f"W1s{c}"], eng="pool")
            self.prep_add(fin, frest)
        for kc in range(8):
            def fin(slot, kc=kc):
                self.dma(stg[slot][:, 0:1040], w1z_d[:, kc, :], f"stg{slot}", [], [f"stg{slot}"], eng="pool")

            def frest(slot, kc=kc):
                self.ts("pool", W1z[:, kc, :], stg[slot][:, 0:1040], gpre[:, kc:kc + 1], ALU.mult, [f"stg{slot}", "gpre"], ["W1z"])
            self.prep_add(fin, frest)
        for hh in range(8):
            def fin(slot, hh=hh):
                self.dma(stg[slot][:, 0:1024], wout_d[:, hh, :], f"stg{slot}", [], [f"stg{slot}"], eng="pool")

            def frest(slot, hh=hh):
                self.ts("pool", Wout[:, hh, :], stg[slot][:, 0:1024], gon[:, 0:1], ALU.mult, [f"stg{slot}", "gon"], ["Wout"])
            self.prep_add(fin, frest)
        self.l1_prep_end = len(self.prep)
        if self.mode == "both":
            self.layer2_prep()
        self.memset("pool", S32[:], 0.0, ["S32_0", "S32_1"])
        self.memset("pool", Sbf[:], 0.0, ["Sbf0", "Sbf1"])
        self.memset("pool", halo[:], 0.0, [f"halo{c}" for c in range(24)])

        PS = self.PS
        sqj = self.al("sqj", [128, 8, 128], BF16)

        def make_F(bi, t0):
            pOI = self.bank(2, 2).rearrange("p (h i) -> p h i", h=8)
            pOA = self.bank(4, 2).rearrange("p (h i) -> p h i", h=8)
            yt = t2[:].rearrange("p h d -> p (h d)")
            r0 = t0 + bi * 128

            def f0():
                self.tt("dve", t1[:], pOI, bc8(sc_oi[:, bi, :]), ALU.mult, ["ps2", "ps3", "sc_oi"], ["t1"])
                self.tt("dve", t2[:], pOA, bc8(sc_o[:, bi, :]), ALU.mult, ["ps4", "ps5", "sc_o"], ["t2"])
                self.dma(xr[:], x_d[r0:r0 + 128, :], "xr", [], ["xr"])

            def f1():
                self.tt("pool", t1[:], t1[:], t2[:], ALU.add, ["t1", "t2"], ["t1"])
                for hh in range(8):
                    self.act(sqj[:, hh, :], t1[:, hh, :], AF.Square, ["t1"], [f"sqj{hh}", f"sso{hh}"], accum_out=sso[:, hh:hh + 1])

            def f2():
                self.act(so[:], sso[:], AF.Ln, [f"sso{q}" for q in range(8)], ["so"], scale=1.0 / 128, bias=EPS)
                self.act(so[:], so[:], AF.Exp, ["so"], ["so"], scale=-0.5)
                self.tt("pool", t2[:], t1[:], zs[:, bi, :].rearrange("p (h d) -> p h d", h=8), ALU.mult, ["t1", f"zs{bi}"], ["t2"])
                self.tt("dve", og[:], t2[:], bc8(so[:, :]), ALU.mult, ["t2", "so"], ["og"])

            def f3():
                pO = self.bank_bf(6).rearrange("p (h d) -> p h d", h=8)
                for hh in range(8):
                    self.tr(pO[:, hh, :], og[:, hh, :], ["og"], ["ps6"])
                self.cp("act", ogT[:], pO, ["ps6"], ["ogT"])

            def f4():
                for half in range(2):
                    for hh in range(8):
                        self.mm(self.bank(6 + half), ogT[:, hh, :], Wout[:, hh, half * 512:(half + 1) * 512], hh == 0, hh == 7,
                                ["ogT", "Wout"], [f"ps{6 + half}"])

            def f5():
                py = self.bank(6, 2)
                self.act(t1[:].rearrange("p h d -> p (h d)"), py, AF.Square, ["ps6", "ps7"], ["t1", "sm4"], accum_out=sm[:, 4:5])
                self.act(sm[:, 5:6], sm[:, 4:5], AF.Ln, ["sm4"], ["sm5"], scale=1.0 / D, bias=EPS)
                self.act(sm[:, 6:7], sm[:, 5:6], AF.Exp, ["sm5"], ["sm6"], scale=-0.5)
                self.stt("dve", yt, py, sm[:, 6:7], gpost[:], ALU.mult, ALU.mult, ["ps6", "ps7", "sm6", "gpost"], ["t2"])

            def f6():
                self.tt("pool", xr[:], yt, xr[:], ALU.add, ["t2", "xr"], ["xr"])
                self.dma(h1_d[r0:r0 + 128, :], xr[:], "h1st", ["xr"], ["h1_dram"], eng="pool")
            return [f0, f1, f2, f3, f4, f5, f6]

        pendF = []
        for g in range(NG):
            t0 = g * TG
            def a1(bi, t0=t0):
                xs = xt[bi % 2]
                xtok = f"xt{bi % 2}"
                so_ = 8 + 3 * (bi % 2)
                r0 = t0 + bi * 128
                self.dma(xs[:], x_d[r0:r0 + 128, :], xtok, [], [xtok])
                self.act(junk[:], xs[:], AF.Square, [xtok], ["junk", f"sm{so_}"], accum_out=sm[:, so_:so_ + 1])
                self.act(sm[:, so_ + 1:so_ + 2], sm[:, so_:so_ + 1], AF.Ln, [f"sm{so_}"], [f"sm{so_ + 1}"], scale=1.0 / D, bias=EPS)
                self.act(sm[:, so_ + 2:so_ + 3], sm[:, so_ + 1:so_ + 2], AF.Exp, [f"sm{so_ + 1}"], [f"sm{so_ + 2}"], scale=-0.5)
                self.ts("dve", xn2[bi % 2][:], xs[:], sm[:, so_ + 2:so_ + 3], ALU.mult, [xtok, f"sm{so_ + 2}"], [f"xn{bi % 2}"])

            def a2(bi):
                pT = self.bank_bf(bi % 2).rearrange("p (k i) -> p k i", k=8)
                for kc in range(8):
                    self.tr(pT[:, kc, :], xn2[bi % 2][:, kc * 128:(kc + 1) * 128], [f"xn{bi % 2}"], [f"ps{bi % 2}"])
                self.cp("act" if bi % 2 == 0 else "dve", xnT[:, :, bi * 128:(bi + 1) * 128], pT, [f"ps{bi % 2}"], ["xnT"])

            a1(0)
            for bi in range(NBG):
                if bi + 1 < NBG:
                    a1(bi + 1)
                a2(bi)
            def proj_part(c, g=g):
                wb = wbuf[c % 3]
                wtok = f"wbuf{c % 3}"
                if g == 0:
                    self.prep_emit_until(c)
                else:
                    self.dma(wb[:], W1s[c].rearrange("p (k n) -> p k n", k=8), f"wld{c % 3}", [f"W1s{c}"], [wtok])
                pb = c % 2
                for kc in range(8):
                    self.mm(self.bank(pb), wb[:, kc, :], xnT[:, kc, :], kc == 0, kc == 7, [wtok, "xnT"], [f"ps{pb}"])
                pcb = pc[c % 3]
                ptok = f"pc{c % 3}"
                self.cp("act" if c % 2 == 0 else "dve", pcb[:, 3:3 + TG], self.bank(pb), [f"ps{pb}"], [ptok])
                self.cp("pool", pcb[:, 0:3], halo[:, c, :], [f"halo{c}"], [ptok])
                self.cp("pool", halo[:, c, :], pcb[:, TG:TG + 3], [ptok], [f"halo{c}"])

            def conv_part(c):
                pcb = pc[c % 3]
                ptok = f"pc{c % 3}"
                cb = 2 + (c % 2)
                for k in range(4):
                    self.mm(self.bank(cb), convd[:, c, k, :], pcb[:, k:k + TG], k == 0, k == 3, [ptok, "convd"], [f"ps{cb}"])
                self.act(qkvT[:, c, :], self.bank(cb), AF.Silu, [f"ps{cb}"], [f"qkvT{c}"])

            for c in range(25):
                if c < 24:
                    proj_part(c)
                if c >= 1:
                    conv_part(c - 1)
            if g == 0:
                self.prep_emit_until(self.l1_prep_end - 9)
            for c in range(16):
                sqb = sq[c % 2]
                self.act(sqb[:], qkvT[:, c, :], AF.Square, [f"qkvT{c}"], [f"sq{c % 2}"])
                for bi in range(NBG):
                    col = 7 * 512 + bi * 16 + c
                    self.mm(PS[:, col:col + 1], sqb[:, bi * 128:(bi + 1) * 128], self.onesb[:, 0:1], True, True,
                            [f"sq{c % 2}", "onesb"], ["ps7"])
            for bi in range(NBG):
                for half in range(2):
                    for kc in range(8):
                        self.mm(self.bank(4 + half), xnT[:, kc, bi * 128:(bi + 1) * 128], W1z[:, kc, half * 512:(half + 1) * 512],
                                kc == 0, kc == 7, ["xnT", "W1z"], [f"ps{4 + half}"])
                for kc in range(8):
                    self.mm(PS[:, 6 * 512:6 * 512 + 16], xnT[:, kc, bi * 128:(bi + 1) * 128], W1z[:, kc, 1024:1040],
                            kc == 0, kc == 7, ["xnT", "W1z"], ["ps6"])
                self.act(zs[:, bi, 0:512], self.bank(4), AF.Silu, ["ps4"], [f"zs{bi}"])
                self.act(zs[:, bi, 512:1024], self.bank(5), AF.Silu, ["ps5"], [f"zs{bi}"])
                self.cp("dve", ab[:, bi, :], PS[:, 6 * 512:6 * 512 + 16], ["ps6"], ["ab"])
            a_ap, b_ap = ab[:, :, 0:8], ab[:, :, 8:16]
            self.tt("dve", tmp8[:], a_ap, dtb[:, :].unsqueeze(1).to_broadcast([128, NBG, 8]), ALU.add, ["ab", "dtb"], ["tmp8"])
            self.act(tmp8[:], tmp8[:], AF.Exp, ["tmp8"], ["tmp8"])
            self.act(tmp8[:], tmp8[:], AF.Ln, ["tmp8"], ["tmp8"], bias=1.0)
            self.tt("dve", g_t[:], tmp8[:], negA[:, :].unsqueeze(1).to_broadcast([128, NBG, 8]), ALU.mult, ["tmp8", "negA"], ["g_t"])
            self.act(lnb[:], b_ap, AF.Exp, ["ab"], ["lnb"], scale=-1.0)
            self.act(lnb[:], lnb[:], AF.Ln, ["lnb"], ["lnb"], bias=1.0)
            self.act(beta[:], lnb[:], AF.Exp, ["lnb"], ["beta"], scale=-1.0)
            gflat = g_t[:].rearrange("p b h -> p (b h)")
            n8 = NBG * 8
            o6 = 6 * 512
            self.mm(PS[:, o6 + 64:o6 + 64 + n8], cst[:, C_LTRI, :], gflat, True, True, ["g_t", "cst"], ["ps6"])
            self.cp("dve", gc[:].rearrange("p b h -> p (b h)"), PS[:, o6 + 64:o6 + 64 + n8], ["ps6"], ["gc"])
            gcflat = gc[:].rearrange("p b h -> p (b h)")
            self.mm(PS[:, o6 + 128:o6 + 128 + n8], cst[:, C_SELOWN, :], gcflat, True, True, ["gc", "cst"], ["ps6"])
            self.mm(PS[:, o6 + 192:o6 + 192 + n8], cst[:, C_SEL0, :], gcflat, True, True, ["gc", "cst"], ["ps6"])
            self.mm(PS[:, o6 + 256:o6 + 256 + n8], cst[:, C_SEL1, :], gcflat, True, True, ["gc", "cst"], ["ps6"])
            self.mm(PS[0:n8, o6 + 320:o6 + 448], gcflat, cst[:, C_ID, :], True, True, ["gc", "cst"], ["ps6"])
            self.cp("dve", glown[:].rearrange("p b h -> p (b h)"), PS[:, o6 + 128:o6 + 128 + n8], ["ps6"], ["glown"])
            for c2 in range(2):
                off = o6 + 192 + 64 * c2
                self.act(dec[:, :, c2, :], PS[:, off:off + n8].rearrange("p (b h) -> p b h", h=8), AF.Exp, ["ps6"], ["dec"])
            self.cp("act", gcT[:], PS[0:n8, o6 + 320:o6 + 448], ["ps6"], ["gcT"])
            self.dma(gcs[g], gcT[:], "gcs", ["gcT"], [f"gcs{g}"])
            self.act(lnss[:], PS[:, 7 * 512:7 * 512 + NBG * 16].rearrange("p (b c) -> p b c", c=16), AF.Ln, ["ps7"], ["lnss"], bias=EPS)
            self.act(rq[:], lnss[:, :, 0:8], AF.Exp, ["lnss"], ["rq"], scale=-0.5)
            self.act(rk[:], lnss[:, :, 8:16], AF.Exp, ["lnss"], ["rk"], scale=-0.5)
            self.act(nk[:], lnss[:, :, 8:16], AF.Exp, ["lnss"], ["nk"], scale=0.5)
            self.ts("dve", lnrk[:], lnss[:, :, 8:16], -0.5, ALU.mult, ["lnss"], ["lnrk"])
            self.act(sc_kg[:], gc[:], AF.Exp, ["gc"], ["sc_kg"])
            self.tt("dve", tmp8[:], glown[:], gc[:], ALU.subtract, ["glown", "gc"], ["tmp8"])
            self.tt("dve", tmp8[:], tmp8[:], lnrk[:], ALU.add, ["tmp8", "lnrk"], ["tmp8"])
            self.act(sc_kd[:], tmp8[:], AF.Exp, ["tmp8"], ["sc_kd"])
            self.tt("dve", sc_vnew[:], beta[:], rk[:], ALU.mult, ["beta", "rk"], ["sc_vnew"])
            self.ts("dve", sc_o[:], rq[:], 128.0 ** -0.5, ALU.mult, ["rq"], ["sc_o"])
            self.tt("dve", sc_oi[:], sc_o[:], sc_kg[:], ALU.mult, ["sc_o", "sc_kg"], ["sc_oi"])
            self.tt("dve", bias1[:], lnrk[:], gc[:], ALU.subtract, ["lnrk", "gc"], ["bias1"])
            self.tt("dve", bias2[:], bias1[:], lnrk[:], ALU.add, ["bias1", "lnrk"], ["bias2"])
            self.tt("dve", bias2[:], bias2[:], lnb[:], ALU.subtract, ["bias2", "lnb"], ["bias2"])
            if g == 0:
                self.prep_emit_until(self.l1_prep_end - 1)
            for bi in range(NBG):
                blk = slice(bi * 128, (bi + 1) * 128)
                if not (g == 0 and bi == 0):
                    self.prep_emit(3)
                if pendF:
                    pendF.pop(0)()
                pk = self.bank_bf(0).rearrange("p (h d) -> p h d", h=8)
                for hh in range(8):
                    self.tr(pk[:, hh, :], qkvT[:, 8 + hh, blk], [f"qkvT{8 + hh}"], ["ps0"])
                self.tt("dve", kg[:], pk, bc8(sc_kg[:, bi, :]), ALU.mult, ["ps0", "sc_kg"], ["kg"])
                self.tt("dve", kd[:], pk, bc8(sc_kd[:, bi, :]), ALU.mult, ["ps0", "sc_kd"], ["kd"])
                pv = self.bank_bf(1).rearrange("p (h d) -> p h d", h=8)
                for hh in range(8):
                    self.tr(pv[:, hh, :], qkvT[:, 16 + hh, blk], [f"qkvT{16 + hh}"], ["ps1"])
                self.tt("dve", vn[:], pv, bc8(nk[:, bi, :]), ALU.mult, ["ps1", "nk"], ["vn"])
                self.dma(X2[:].rearrange("p h i -> p (h i)"),
                         gcs[g, bi * 8:(bi + 1) * 8, :].rearrange("(o h) i -> o (h i)", o=1).partition_broadcast(128),
                         "e2b", [f"gcs{g}"], [f"X2_{q}" for q in range(8)])
                self.tt("pool", X1[:], X2[:], cst[:, C_MASKI, :].unsqueeze(1).to_broadcast([128, 8, 128]), ALU.add, [f"X2_{q}" for q in range(8)] + ["cst"], [f"X1_{q}" for q in range(8)])
                self.tt("pool", X2[:], X2[:], cst[:, C_MASKS, :].unsqueeze(1).to_broadcast([128, 8, 128]), ALU.add, [f"X2_{q}" for q in range(8)] + ["cst"], [f"X2_{q}" for q in range(8)])
                for hh in range(8):
                    self.act(X1[:, hh, :], X1[:, hh, :], AF.Exp, [f"X1_{hh}", "bias1"], [f"X1_{hh}"], bias=bias1[:, bi, hh:hh + 1])
                for hh in range(8):
                    self.act(X2[:, hh, :], X2[:, hh, :], AF.Exp, [f"X2_{hh}", "bias2"], [f"X2_{hh}"], bias=bias2[:, bi, hh:hh + 1])
                pG = self.bank(2, 2).rearrange("p (h i) -> p h i", h=8)
                pQK = self.bank(4, 2).rearrange("p (h i) -> p h i", h=8)
                for hh in range(8):
                    self.mm(pG[:, hh, :], qkvT[:, 8 + hh, blk], qkvT[:, 8 + hh, blk], True, True, [f"qkvT{8 + hh}"], [f"ps{2 + hh // 4}"])
                for hh in range(8):
                    self.mm(pQK[:, hh, :], qkvT[:, 8 + hh, blk], qkvT[:, hh, blk], True, True, [f"qkvT{8 + hh}", f"qkvT{hh}"], [f"ps{4 + hh // 4}"])
                Q0, P0, T0 = Qm[0], Pm[0], TTm[0]
                hs_ = [slice(0, 4), slice(4, 8)]
                for hf in range(2):
                    self.stt("dve", Q0[:, hs_[hf], :], pG[:, hs_[hf], :], -1.0, X2[:, hs_[hf], :], ALU.mult, ALU.mult,
                             [f"ps{2 + hf}"] + [f"X2_{q}" for q in range(4 * hf, 4 * hf + 4)], [f"Qm0_{hf}"])
                self.tt("dve", attnT[:], pQK, X1[:], ALU.mult, ["ps4", "ps5"] + [f"X1_{q}" for q in range(8)], ["attnT"])
                pP = self.bank_bf(0).rearrange("p (h d) -> p h d", h=8)
                for hh in range(8):
                    self.tr(pP[:, hh, :], Q0[:, hh, :], [f"Qm0_{hh // 4}"], ["ps0"])
                for hf in range(2):
                    self.cp("act", P0[:, hs_[hf], :], pP[:, hs_[hf], :], ["ps0"], [f"Pm0_{hf}"])
                    self.tt("pool", T0[:, hs_[hf], :], Q0[:, hs_[hf], :], idb[:].unsqueeze(1).to_broadcast([128, 4, 128]), ALU.add,
                            [f"Qm0_{hf}", "idb"], [f"TTm0_{hf}"])
                for k in range(1, 6):
                    pi, ci = (k - 1) % 2, k % 2
                    Pp, Qp, Tp = Pm[pi], Qm[pi], TTm[pi]
                    Pc, Qc, Tc = Pm[ci], Qm[ci], TTm[ci]
                    pPk = self.bank(0, 2).rearrange("p (h i) -> p h i", h=8)
                    pQk = self.bank(2, 2).rearrange("p (h i) -> p h i", h=8)
                    pTk = self.bank(4, 2).rearrange("p (h i) -> p h i", h=8)
                    for hf in range(2):
                        for hh in range(4 * hf, 4 * hf + 4):
                            self.mm(pPk[:, hh, :], Qp[:, hh, :], Pp[:, hh, :], True, True, [f"Qm{pi}_{hf}", f"Pm{pi}_{hf}"], [f"ps{hf}"])
                        if k < 5:
                            for hh in range(4 * hf, 4 * hf + 4):
                                self.mm(pQk[:, hh, :], Pp[:, hh, :], Qp[:, hh, :], True, True, [f"Qm{pi}_{hf}", f"Pm{pi}_{hf}"], [f"ps{2 + hf}"])
                    for hf in range(2):
                        self.cp("act", Pc[:, hs_[hf], :], pPk[:, hs_[hf], :], [f"ps{hf}"], [f"Pm{ci}_{hf}"])
                        if k < 5:
                            self.cp("dve", Qc[:, hs_[hf], :], pQk[:, hs_[hf], :], [f"ps{2 + hf}"], [f"Qm{ci}_{hf}"])
                    for hf in range(2):
                        for hh in range(4 * hf, 4 * hf + 4):
                            self.mm(pTk[:, hh, :], idb[:], Tp[:, hh, :], True, False, ["idb", f"TTm{pi}_{hf}"], [f"ps{4 + hf}"])
                            self.mm(pTk[:, hh, :], Pc[:, hh, :], Tp[:, hh, :], False, True, [f"Pm{ci}_{hf}", f"TTm{pi}_{hf}"], [f"ps{4 + hf}"])
                    for hf in range(2):
                        self.cp("dve" if hf == 0 else "act", Tc[:, hs_[hf], :], pTk[:, hs_[hf], :], [f"ps{4 + hf}"], [f"TTm{ci}_{hf}"])
                    if pendF:
                        pendF.pop(0)()
                while pendF:
                    pendF.pop(0)()
                TT = TTm[1]
                pW = self.bank(6, 2).rearrange("p (h i) -> p h i", h=8)
                for hh in range(8):
                    self.mm(pW[:, hh, :], kg[:, hh, :], TT[:, hh, :], True, True, ["kg", f"TTm1_{hh // 4}"], [f"ps{6 + hh // 4}"])
                self.P.op("act", lambda e, o=nwT[:], i=pW: e.mul(out=o, in_=i, mul=-1.0), ["ps6", "ps7"], ["nwT"])
                pV = self.bank(0, 2).rearrange("p (h i) -> p h i", h=8)
                pOI = self.bank(2, 2).rearrange("p (h i) -> p h i", h=8)
                pOA = self.bank(4, 2).rearrange("p (h i) -> p h i", h=8)
                pDS = self.bank(6, 2).rearrange("p (h i) -> p h i", h=8)
                for c2 in range(2):
                    r = slice(c2 * 64, (c2 + 1) * 64)
                    tokr = slice(bi * 128 + c2 * 64, bi * 128 + (c2 + 1) * 64)
                    for hf in range(2):
                        for hh in range(4 * hf, 4 * hf + 4):
                            self.mm(pV[r, hh, :], TT[r, hh, r], vn[r, hh, :], True, False, [f"TTm1_{hf}", "vn"], [f"ps{hf}"])
                            self.mm(pV[r, hh, :], nwT[:, hh, r], Sbf[:, hh, :], False, True, ["nwT", f"Sbf{hf}"], [f"ps{hf}"])
                    for hf in range(2):
                        hsl = slice(4 * hf, 4 * hf + 4)
                        self.tt("dve", vnew[r, hsl, :], pV[r, hsl, :], sc_vnew[r, bi, hsl].unsqueeze(2).to_broadcast([64, 4, 128]), ALU.mult,
                                [f"ps{hf}", "sc_vnew"], [f"vnew{hf}"])
                        self.tt("pool", Sd[:, hsl, :], S32[:, hsl, :], dec[:, bi, c2, hsl].unsqueeze(2).to_broadcast([128, 4, 128]), ALU.mult,
                                [f"S32_{hf}", "dec"], [f"Sd{hf}"])
                    for hf in range(2):
                        for hh in range(4 * hf, 4 * hf + 4):
                            self.mm(pDS[:, hh, :], kd[r, hh, :], vnew[r, hh, :], True, True, ["kd", f"vnew{hf}"], [f"ps{6 + hf}"])
                    for hf in range(2):
                        for hh in range(4 * hf, 4 * hf + 4):
                            self.mm(pOI[r, hh, :], qkvT[:, hh, tokr], Sbf[:, hh, :], True, True, [f"qkvT{hh}", f"Sbf{hf}"], [f"ps{2 + hf}"])
                    for hf in range(2):
                        hsl = slice(4 * hf, 4 * hf + 4)
                        self.tt("dve", S32[:, hsl, :], Sd[:, hsl, :], pDS[:, hsl, :], ALU.add, [f"Sd{hf}", f"ps{6 + hf}"], [f"S32_{hf}"])
                        self.cp("act", Sbf[:, hsl, :], S32[:, hsl, :], [f"S32_{hf}"], [f"Sbf{hf}"])
                    for hf in range(2):
                        for hh in range(4 * hf, 4 * hf + 4):
                            self.mm(pOA[r, hh, :], attnT[r, hh, r], vnew[r, hh, :], True, True, ["attnT", f"vnew{hf}"], [f"ps{4 + hf}"])
                pendF.extend(make_F(bi, t0))
            while pendF:
                pendF.pop(0)()
        return "h1_dram"


    def layer2_prep(self):
        P = self.P
        stg = self.stg
        w2_d = self.din("w2", [8, 128, 8, 513])
        kvn_d = self.din("kvn", [128, 8])
        fpn_d = self.din("fpn", [128, 8])
        wout2_d = self.din("wout2", [128, 8, 1024])
        self.gk_d = self.din("gk", [128, 1])
        self.gq_d = self.din("gq", [128, 1])
        self.fb_d = self.din("fb_b", [128, 8])
        self.gpost2_d = self.din("gpost2", [1, 1024])
        self.W2s = self.dscr("W2s", [8, 128, 8 * 513], BF16)
        self.Wout2s = self.dscr("Wout2s", [128, 8 * 1024], BF16)
        kvn = P.sbuf("kvn", [128, 8], F32)
        fpn = P.sbuf("fpn", [128, 8], F32)
        wst = [P.sbuf(f"wst{i}", [128, 1024], BF16) for i in range(2)]
        self.dma(kvn[:], kvn_d, "iniK", [], ["kvn"])
        self.dma(fpn[:], fpn_d, "iniK", [], ["fpn"])
        self.join(["kvn", "fpn"], 10)
        for h in range(8):
            for kc in range(8):
                def fin(slot, h=h, kc=kc):
                    self.dma(stg[slot][:, 0:513], w2_d[h, :, kc, :], f"stg{slot}", [], [f"stg{slot}"], eng="pool")

                def frest(slot, h=h, kc=kc):
                    sg, wb = stg[slot], wst[slot]
                    stok, wtok = f"stg{slot}", f"wst{slot}"
                    self.ts("pool", wb[:, 0:128], sg[:, 0:128], kvn[:, kc:kc + 1], ALU.mult, [stok, "kvn"], [wtok])
                    self.ts("pool", wb[:, 128:384], sg[:, 128:384], fpn[:, kc:kc + 1], ALU.mult, [stok, "fpn"], [wtok])
                    self.ts("pool", wb[:, 384:513], sg[:, 384:513], kvn[:, kc:kc + 1], ALU.mult, [stok, "kvn"], [wtok])
                    self.dma(self.W2s[h, :, kc * 513:(kc + 1) * 513], wb[:, 0:513], f"w2s{slot}", [wtok], [f"W2s{h}"], eng="pool")
                self.prep_add(fin, frest)
        for hh in range(8):
            def fin(slot, hh=hh):
                self.dma(stg[slot][:, 0:1024], wout2_d[:, hh, :], f"stg{slot}", [], [f"stg{slot}"], eng="pool")

            def frest(slot, hh=hh):
                sg, wb = stg[slot], wst[slot]
                self.cp("pool", wb[:, 0:1024], sg[:, 0:1024], [f"stg{slot}"], [f"wst{slot}"])
                self.dma(self.Wout2s[:, hh * 1024:(hh + 1) * 1024], wb[:, 0:1024], f"w2s{slot}", [f"wst{slot}"], ["Wout2s"], eng="pool")
            self.prep_add(fin, frest)

    def layer2(self, h1_d, h1_tok, out_d):
        P = self.P
        L, TG, NG, NB = self.L, self.TG, self.NG, self.NB
        cst = self.cst
        PS = self.PS
        og2s = self.dscr("og2s", [8, 128, L], BF16)
        cs = self.dscr("cs", [8, NB, 128], F32)
        self.arena_reset()
        ht = [self.al(f"ht{i}", [128, 1024], F32) for i in range(2)]
        junk = self.al("junk2", [128, 1024], BF16)
        sm = self.al("sm2", [128, 16], F32)
        phc_mark = self.a_off
        hnT = self.al("hnT", [128, 8, L], BF16)
        W2h = [self.al(f"W2h{i}", [128, 8, 513], BF16) for i in range(2)]
        KT = self.al("KT", [128, L], BF16)
        QT = self.al("QT", [128, L], BF16)
        ZT = self.al("ZT", [128, L], BF16)
        V = self.al("V", [128, NB, 128], BF16)
        hn2 = [self.al(f"hn{i}", [128, 1024], BF16) for i in range(2)]
        gk = self.al("gk", [128, 1], F32)
        gq = self.al("gq", [128, 1], F32)
        nfb = self.al("nfb", [128, 8], F32)
        sqb = [self.al(f"sqb{i}", [128, 512], BF16) for i in range(2)]
        lnt = [self.al(f"lnt{i}", [128, 512], F32) for i in range(2)]
        fcol = self.al("fcol", [128, NB], F32)
        lf = self.al("lf", [128, NB], F32)
        pre = [self.al(f"pre{i}", [128, NB], F32) for i in range(2)]
        ctok = self.al("ctok", [128, NB], F32)
        negc = self.al("negc", [128, NB], F32)
        cT = self.al("cT", [NB, 128], F32)
        cqb = [self.al(f"cqb{i}", [128, 512], F32) for i in range(2)]
        cqm = [self.al(f"cqm{i}", [128, 4, 128], F32) for i in range(2)]
        tt_ = [self.al(f"tt{i}", [128, 512], F32) for i in range(4)]
        pp_ = [self.al(f"pp{i}", [128, 512], BF16) for i in range(4)]
        SBK = [0, 1, 2, 7]
        rl = self.al("rl", [128, 512], F32)
        o1 = self.al("o1", [128, 512], F32)
        ogt = [self.al(f"ogt{i}", [128, 512], BF16) for i in range(2)]

        self.dma(gk[:], self.gk_d, "iniB", [], ["gk"])
        self.dma(gq[:], self.gq_d, "iniB", [], ["gq"])
        self.dma(nfb[:], self.fb_d, "iniB", [], ["nfb"])
        self.join(["gk", "gq", "nfb"], 11)
        self.ts("dve", gq[:], gq[:], 128.0 ** -0.5, ALU.mult, ["gq"], ["gq"])
        self.ts("dve", nfb[:], nfb[:], -1.0, ALU.mult, ["nfb"], ["nfb"])

        def a1(b):
            hs = ht[b % 2]
            htok = f"ht{b % 2}"
            so_ = 3 * (b % 2)
            self.dma(hs[:], h1_d[b * 128:(b + 1) * 128, :], htok, [h1_tok], [htok])
            self.act(junk[:], hs[:], AF.Square, [htok], ["junk2", f"s2a{so_}"], accum_out=sm[:, so_:so_ + 1])
            self.act(sm[:, so_ + 1:so_ + 2], sm[:, so_:so_ + 1], AF.Ln, [f"s2a{so_}"], [f"s2a{so_ + 1}"], scale=1.0 / D, bias=EPS)
            self.act(sm[:, so_ + 2:so_ + 3], sm[:, so_ + 1:so_ + 2], AF.Exp, [f"s2a{so_ + 1}"], [f"s2a{so_ + 2}"], scale=-0.5)
            self.ts("dve", hn2[b % 2][:], hs[:], sm[:, so_ + 2:so_ + 3], ALU.mult, [htok, f"s2a{so_ + 2}"], [f"hn{b % 2}"])

        def a2(b):
            pT = self.bank_bf(b % 2).rearrange("p (k i) -> p k i", k=8)
            for kc in range(8):
                self.tr(pT[:, kc, :], hn2[b % 2][:, kc * 128:(kc + 1) * 128], [f"hn{b % 2}"], [f"ps{b % 2}"])
            self.cp("act" if b % 2 == 0 else "dve", hnT[:, :, b * 128:(b + 1) * 128], pT, [f"ps{b % 2}"], ["hnT"])

        a1(0)
        for b in range(NB):
            if b + 1 < NB:
                a1(b + 1)
            a2(b)

        nproj = 0
        nsc = 0
        for h in range(8):
            Wh = W2h[h % 2]
            wtok = f"W2h{h % 2}"
            self.dma(Wh[:].rearrange("p k n -> p (k n)"), self.W2s[h], wtok, [f"W2s{h}"], [wtok])
            for b in range(NB):
                vb = 4 + (b % 2)
                for kc in range(8):
                    self.mm(self.bank(vb)[:, 0:129], hnT[:, kc, b * 128:(b + 1) * 128], Wh[:, kc, 384:513], kc == 0, kc == 7,
                            [wtok, "hnT"], [f"ps{vb}"])
                self.cp("act", V[:, b, :], self.bank(vb)[:, 0:128], [f"ps{vb}"], ["V"])
                self.cp("dve", fcol[:, b:b + 1], self.bank(vb)[:, 128:129], [f"ps{vb}"], ["fcol"])
            self.act(lf[:], fcol[:], AF.Exp, ["fcol", "nfb"], ["lf"], scale=-1.0, bias=nfb[:, h:h + 1])
            self.act(lf[:], lf[:], AF.Ln, ["lf"], ["lf"], bias=1.0)
            self.ts("dve", lf[:], lf[:], -1.0, ALU.mult, ["lf"], ["lf"])
            o6 = 6 * 512
            self.mm(PS[:, o6:o6 + NB], cst[:, C_LFULL, :], lf[:], True, True, ["lf", "cst"], ["ps6"])
            self.mm(PS[:, o6 + 64:o6 + 64 + NB], cst[:, C_ONES, :], lf[:], True, True, ["lf", "cst"], ["ps6"])
            self.cp("dve", pre[0][:], PS[:, o6 + 64:o6 + 64 + NB], ["ps6"], ["pre0"])
            cur = 0
            st_ = 1
            while st_ < NB:
                nxt = 1 - cur
                self.cp("dve", pre[nxt][:, 0:st_], pre[cur][:, 0:st_], [f"pre{cur}"], [f"pre{nxt}"])
                self.tt("dve", pre[nxt][:, st_:NB], pre[cur][:, st_:NB], pre[cur][:, 0:NB - st_], ALU.add, [f"pre{cur}"], [f"pre{nxt}"])
                cur = nxt
                st_ *= 2
            self.tt("dve", ctok[:], pre[cur][:], PS[:, o6 + 64:o6 + 64 + NB], ALU.subtract, [f"pre{cur}", "ps6"], ["ctok"])
            self.tt("dve", ctok[:], ctok[:], PS[:, o6:o6 + NB], ALU.add, ["ctok", "ps6"], ["ctok"])
            self.ts("dve", negc[:], ctok[:], -1.0, ALU.mult, ["ctok"], ["negc"])
            self.mm(PS[0:NB, o6 + 128:o6 + 256], ctok[:], cst[:, C_ID, :], True, True, ["ctok", "cst"], ["ps6"])
            self.cp("act", cT[:], PS[0:NB, o6 + 128:o6 + 256], ["ps6"], ["cT"])
            self.dma(cs[h], cT[:], "cs", ["cT"], [f"cs{h}"])
            ptiles = [(0, tg) for tg in range(NG)] + [(1, tg) for tg in range(NG)] + [(2, tg) for tg in range(NG)]

            def p_part1(i, Wh=Wh, wtok=wtok):
                kind, tg = ptiles[i]
                cols = slice(tg * TG, (tg + 1) * TG)
                pb = i % 2
                c0 = 128 * kind
                for kc in range(8):
                    self.mm(self.bank(pb)[:, 0:TG], Wh[:, kc, c0:c0 + 128], hnT[:, kc, cols], kc == 0, kc == 7, [wtok, "hnT"], [f"ps{pb}"])
                if kind < 2:
                    self.act(sqb[pb][:, 0:TG], self.bank(pb)[:, 0:TG], AF.Square, [f"ps{pb}"], [f"sqb{pb}"])

            def p_part2(i):
                kind, tg = ptiles[i]
                cols = slice(tg * TG, (tg + 1) * TG)
                pb = i % 2
                if kind == 2:
                    self.act(ZT[:, cols], self.bank(pb)[:, 0:TG], AF.Silu, [f"ps{pb}"], ["ZT"])
                    return
                dst, gain = (KT, gk) if kind == 0 else (QT, gq)
                self.mm(self.bank(2 + pb)[:, 0:TG], self.onesb[:], sqb[pb][:, 0:TG], True, True, [f"sqb{pb}", "onesb"], [f"ps{2 + pb}"])
                ln_ = lnt[pb]
                self.act(ln_[:, 0:TG], self.bank(2 + pb)[:, 0:TG], AF.Ln, [f"ps{2 + pb}"], [f"lnt{pb}"], scale=1.0 / 128, bias=EPS)
                self.act(ln_[:, 0:TG], ln_[:, 0:TG], AF.Exp, [f"lnt{pb}"], [f"lnt{pb}"], scale=-0.5)
                self.stt("dve", dst[:, cols], self.bank(pb)[:, 0:TG], gain[:, 0:1], ln_[:, 0:TG], ALU.mult, ALU.mult,
                         [f"ps{pb}", f"lnt{pb}", "gk", "gq"], ["KT" if kind == 0 else "QT"])

            for i in range(len(ptiles) + 1):
                if i < len(ptiles):
                    p_part1(i)
                if i >= 1:
                    p_part2(i - 1)
            nqb = TG // 128
            tiles = [(jq, kb) for jq in range(NG) for kb in range(nqb * jq + nqb)]
            LOOK = 3

            def pre_q(jq, h=h):
                q0 = jq * TG
                cq = cqb[jq % 2]
                ctk = f"cqb{jq % 2}"
                self.dma(cq[:, 0:TG], cs[h].rearrange("b i -> (b i)")[q0:q0 + TG].rearrange("(o n) -> o n", o=1).partition_broadcast(128),
                         ctk, [f"cs{h}"], [ctk])
                for d_ in range(nqb):
                    self.tt("pool", cqm[jq % 2][:, d_, :], cq[:, d_ * 128:(d_ + 1) * 128], cst[:, C_MTRI, :], ALU.add,
                            [ctk, "cst"], [f"cqm{jq % 2}_{d_}"])

            def s_mm(n):
                jq, kb = tiles[n]
                q0 = jq * TG
                d_ = kb - nqb * jq
                c0 = 0 if d_ < 0 else d_ * 128
                si = n % 4
                sbk = SBK[si]
                self.mm(self.bank(sbk)[:, c0:TG], KT[:, kb * 128:(kb + 1) * 128], QT[:, q0 + c0:q0 + TG], True, True,
                        ["KT", "QT"], [f"ps{sbk}"])

            def rest(n, h=h):
                jq, kb = tiles[n]
                q0 = jq * TG
                d_ = kb - nqb * jq
                c0 = 0 if d_ < 0 else d_ * 128
                si = n % 4
                sbk = SBK[si]
                tt = tt_[si]
                pp = pp_[si]
                cq = cqb[jq % 2]
                ctk = f"cqb{jq % 2}"
                ob = 3 + (jq % 2)
                lb = 5 + (jq % 2)
                nk_ = nqb * jq + nqb
                if d_ >= 0:
                    self.tt("dve", tt[:, c0:c0 + 128], self.bank(sbk)[:, c0:c0 + 128], cqm[jq % 2][:, d_, :], ALU.add,
                            [f"ps{sbk}", f"cqm{jq % 2}_{d_}"], [f"tt{si}"])
                    if c0 + 128 < TG:
                        self.tt("dve", tt[:, c0 + 128:TG], self.bank(sbk)[:, c0 + 128:TG], cq[:, c0 + 128:TG], ALU.add,
                                [f"ps{sbk}", ctk], [f"tt{si}"])
                else:
                    self.tt("dve", tt[:, 0:TG], self.bank(sbk)[:, 0:TG], cq[:, 0:TG], ALU.add, [f"ps{sbk}", ctk], [f"tt{si}"])
                self.act(pp[:, c0:TG], tt[:, c0:TG], AF.Exp, [f"tt{si}", "negc"], [f"pp{si}"], bias=negc[:, kb:kb + 1])
                self.mm(self.bank(ob)[:, c0:TG], V[:, kb, :], pp[:, c0:TG], kb == 0, kb == nk_ - 1, ["V", f"pp{si}"], [f"ps{ob}"])
                self.mm(self.bank(lb)[:, c0:TG], self.onesb[:], pp[:, c0:TG], kb == 0, kb == nk_ - 1, ["onesb", f"pp{si}"], [f"ps{lb}"])

            def post(jq, h=h):
                q0 = jq * TG
                ob = 3 + (jq % 2)
                lb = 5 + (jq % 2)
                self.act(rl[:, 0:TG], self.bank(lb)[:, 0:TG], AF.Ln, [f"ps{lb}"], ["rl"])
                self.act(rl[:, 0:TG], rl[:, 0:TG], AF.Exp, ["rl"], ["rl"], scale=-1.0)
                self.tt("dve", o1[:, 0:TG], self.bank(ob)[:, 0:TG], rl[:, 0:TG], ALU.mult, [f"ps{ob}", "rl"], ["o1"])
                og_ = ogt[jq % 2]
                self.tt("pool", og_[:, 0:TG], o1[:, 0:TG], ZT[:, q0:q0 + TG], ALU.mult, ["o1", "ZT"], [f"ogt{jq % 2}"])
                self.dma(og2s[h, :, q0:q0 + TG], og_[:, 0:TG], f"ogst{jq % 2}", [f"ogt{jq % 2}"], ["og2s"], eng="pool")

            pre_q(0)
            if NG > 1:
                pre_q(1)
            for n in range(len(tiles) + LOOK + 2):
                if n < len(tiles):
                    s_mm(n)
                m = n - LOOK
                if m >= 0 and m < len(tiles):
                    rest(m)
                m2 = m - 2
                if m2 >= 0 and m2 < len(tiles):
                    jq_m, kb_m = tiles[m2]
                    if kb_m == nqb * jq_m + nqb - 1:
                        post(jq_m)
                        if jq_m + 2 < NG:
                            pre_q(jq_m + 2)

        self.barrier("l2c")
        self.a_off = phc_mark
        Wout2 = self.al("Wout2", [128, 8, 1024], BF16)
        ogb = [self.al(f"ogb{i}", [128, 8, 128], BF16) for i in range(3)]
        yts = [self.al(f"yt2_{i}", [128, 1024], F32) for i in range(2)]
        smc = self.al("smc", [128, 16], F32)
        gpost = self.al("gpost2", [128, 1024], F32)
        self.dma(gpost[:], self.gpost2_d.partition_broadcast(128), "iniC", [], ["gpost2"])
        for hh in range(8):
            self.dma(Wout2[:, hh, :], self.Wout2s[:, hh * 1024:(hh + 1) * 1024], "iniC", ["Wout2s"], [f"Wout2_{hh}"],
                     eng="sp")
        self.join(["gpost2"] + [f"Wout2_{q}" for q in range(8)], 12)
        for b in range(NB):
            ob_ = ogb[b % 3]
            otok = f"ogb{b % 3}"
            self.dma(ob_[:], og2s[:, :, b * 128:(b + 1) * 128].rearrange("h p t -> p h t"), otok, ["og2s"], [otok])
            bp = 2 * (b % 4)
            for half in range(2):
                for hh in range(8):
                    self.mm(self.bank(bp + half), ob_[:, hh, :], Wout2[:, hh, half * 512:(half + 1) * 512], hh == 0, hh == 7,
                            [otok, f"Wout2_{hh}"], [f"ps{bp + half}"])
            py = self.bank(bp, 2)
            so_ = 4 * (b % 4)
            self.act(junk[:], py, AF.Square, [f"ps{bp}", f"ps{bp + 1}"], ["junk2", f"s2c{so_}"], accum_out=smc[:, so_:so_ + 1])
            self.act(smc[:, so_ + 1:so_ + 2], smc[:, so_:so_ + 1], AF.Ln, [f"s2c{so_}"], [f"s2c{so_ + 1}"], scale=1.0 / D, bias=EPS)
            self.act(smc[:, so_ + 2:so_ + 3], smc[:, so_ + 1:so_ + 2], AF.Exp, [f"s2c{so_ + 1}"], [f"s2c{so_ + 2}"], scale=-0.5)
            yt = yts[b % 2]
            self.stt("dve", yt[:], py, smc[:, so_ + 2:so_ + 3], gpost[:], ALU.mult, ALU.mult,
                     [f"ps{bp}", f"ps{bp + 1}", f"s2c{so_ + 2}", "gpost2"], [f"yt2_{b % 2}"])
            hs = ht[b % 2]
            htok = f"ht{b % 2}"
            self.dma(hs[:], h1_d[b * 128:(b + 1) * 128, :], htok, [h1_tok], [htok])
            self.tt("pool", hs[:], yt[:], hs[:], ALU.add, [f"yt2_{b % 2}", htok], [htok])
            self.dma(out_d[b * 128:(b + 1) * 128, :], hs[:], f"ost{b % 2}", [htok], ["out_dram"], eng="pool")
        return "out_dram"

    def build(self):
        L = self.L
        self.setup_common()
        if self.mode == "l1":
            x_d = self.din("x", [L, D])
            h1_d = self.dout("h1", [L, D])
            tok = self.layer1(x_d, h1_d)
            self.final_tokens.append(tok)
        elif self.mode == "l2":
            h1_d = self.din("h1", [L, D])
            out_d = self.dout("out", [L, D])
            self.layer2_prep()
            self.prep_emit(len(self.prep))
            tok = self.layer2(h1_d, "h1_in", out_d)
            self.final_tokens.append(tok)
        else:
            x_d = self.din("x", [L, D])
            h1_d = self.dscr("h1", [L, D])
            out_d = self.dout("out", [L, D])
            tok1 = self.layer1(x_d, h1_d)
            self.prep_emit(len(self.prep))
            self.barrier("l12")
            tok = self.layer2(h1_d, tok1, out_d)
            self.final_tokens.append(tok)
        self.P.finalize(final_wait_tokens=self.final_tokens)
        return self.nc


def prep_shared_inputs(inp):
    f = lambda a: np.ascontiguousarray(a, dtype=np.float32)
    w_in = inp["gdn_w_in"][0]
    wq = w_in[:, :3072].reshape(8, 128, 24, 128).transpose(2, 1, 0, 3)
    wz = w_in[:, 3072:4112].reshape(8, 128, 1040).transpose(1, 0, 2)
    sh = {
        "consts": make_consts(),
        "w1qkv": f(wq),
        "w1z": f(wz),
        "gpre": f(inp["gdn_pre_norm"][0].reshape(8, 128).T),
        "cw": f(inp["gdn_conv_w"][0].reshape(4, 24, 128).transpose(2, 1, 0)),
        "alog_b": f(np.broadcast_to(inp["gdn_a_log"][0][None, :], (128, 8))),
        "dtb_b": f(np.broadcast_to(inp["gdn_dt_bias"][0][None, :], (128, 8))),
        "gonorm": f(inp["gdn_o_norm"][0].reshape(128, 1)),
        "wout1": f(inp["gdn_w_out"][0].reshape(8, 128, 1024).transpose(1, 0, 2)),
        "gpost1": f(inp["gdn_post_norm"][0].reshape(1, 1024)),
    }
    kvw = inp["kv_w"]
    fw = inp["fox_w_in"][0]
    w2 = np.zeros((8, 1024, 513), np.float32)
    for h in range(8):
        w2[h, :, 0:128] = kvw[:, h * 128:(h + 1) * 128]
        w2[h, :, 128:256] = fw[:, h * 128:(h + 1) * 128]
        w2[h, :, 256:384] = fw[:, 1024 + h * 128:1024 + (h + 1) * 128]
        w2[h, :, 384:512] = kvw[:, 1024 + h * 128:1024 + (h + 1) * 128]
        w2[h, :, 512] = kvw[:, 2048 + h]
    sh.update({
        "w2": f(w2.reshape(8, 8, 128, 513).transpose(0, 2, 1, 3)),
        "kvn": f(inp["kv_norm"].reshape(8, 128).T),
        "fpn": f(inp["fox_pre_norm"][0].reshape(8, 128).T),
        "wout2": f(inp["fox_w_out"][0].reshape(8, 128, 1024).transpose(1, 0, 2)),
        "gk": f(inp["kv_k_norm"].reshape(128, 1)),
        "gq": f(inp["fox_q_norm"][0].reshape(128, 1)),
        "fb_b": f(np.broadcast_to(inp["kv_forget_bias"][None, :], (128, 8))),
        "gpost2": f(inp["fox_post_norm"][0].reshape(1, 1024)),
    })
    return sh


FUSED = True
SEQ = 4096
NCORES = 8


def _run(mode, per_core_extra, sh, out_name):
    b = Builder(SEQ, mode=mode)
    nc = b.build()
    base = {k: v for k, v in sh.items() if k in b.dram}
    in_maps = []
    for c in range(NCORES):
        m = dict(base)
        m.update(per_core_extra[c])
        in_maps.append(m)
    res = run_bass_kernel_spmd(nc, in_maps, core_ids=list(range(NCORES)))
    return [np.asarray(r[out_name]) for r in res.results]


def kernel(**inputs):
    inp = {k: np.asarray(v) for k, v in inputs.items()}
    sh = prep_shared_inputs(inp)
    x = np.ascontiguousarray(inp["x"], dtype=np.float32)
    if FUSED:
        outs = _run("both", [{"x": x[c]} for c in range(NCORES)], sh, "out")
    else:
        h1 = _run("l1", [{"x": x[c]} for c in range(NCORES)], sh, "h1")
        outs = _run("l2", [{"h1": np.ascontiguousarray(h1[c])} for c in range(NCORES)], sh, "out")
    return np.stack(outs, axis=0).astype(np.float32)
```

```python
import contextlib
import numpy as np
import concourse.bass as bass
import concourse.mybir as mybir
from concourse.bass_utils import run_bass_kernel_spmd

F32 = mybir.dt.float32
BF16 = mybir.dt.bfloat16
AF = mybir.ActivationFunctionType
ALU = mybir.AluOpType

D = 1024
H = 8
EPS = 1e-6
NEG = -30000.0
ENGS = ("pe", "act", "dve", "pool", "sp")

SYNC_ALL = False
BAR_DMA = True


class Op:
    __slots__ = ("eng", "fn", "reads", "writes", "is_dma", "chan", "waits", "all_dma")


class Prog:
    def __init__(self, nc):
        self.nc = nc
        self.ops = []
        self.stack = contextlib.ExitStack()

    def sbuf(self, name, shape, dtype):
        return self.stack.enter_context(self.nc.sbuf_tensor("sb_" + name, list(shape), dtype))

    def psum(self, name, shape, dtype):
        return self.stack.enter_context(self.nc.psum_tensor(name, list(shape), dtype))

    def op(self, eng, fn, reads=(), writes=()):
        o = Op()
        o.eng, o.fn, o.reads, o.writes = eng, fn, tuple(reads), tuple(writes)
        o.is_dma, o.chan, o.all_dma = False, None, False
        self.ops.append(o)
        return o

    def dma(self, eng, fn, chan, reads=(), writes=()):
        o = self.op(eng, fn, reads, writes)
        o.is_dma, o.chan = True, chan
        return o

    def finalize(self, final_wait_tokens=()):
        nc = self.nc
        ops = self.ops
        last_w, readers = {}, {}
        eng_know = {e: {} for e in ENGS}
        cnt = {}
        op_sig = [None] * len(ops)
        op_know = [None] * len(ops)
        waited_on = set()
        last_dma = {}
        for i, o in enumerate(ops):
            deps = set()
            if o.all_dma:
                deps.update(last_dma.values())
            if o.is_dma:
                last_dma[o.chan] = i
            for t in o.reads:
                w = last_w.get(t)
                if w is not None:
                    deps.add(w)
            for t in o.writes:
                w = last_w.get(t)
                if w is not None:
                    deps.add(w)
                for r in readers.get(t, ()):
                    deps.add(r)
            know = eng_know[o.eng]
            best = {}
            for d in deps:
                if d == i:
                    continue
                do = ops[d]
                sk, ordv = op_sig[d]
                if (not do.is_dma) and do.eng == o.eng and not o.is_dma:
                    if o.eng in ("pe", "sp"):
                        continue
                    if not SYNC_ALL and not any(t in do.writes for t in o.reads):
                        continue
                if know.get(sk, 0) >= ordv:
                    continue
                if sk not in best or op_sig[best[sk]][1] < ordv:
                    best[sk] = d
            o.waits = []
            for sk, d in best.items():
                if know.get(sk, 0) >= op_sig[d][1]:
                    continue
                o.waits.append(d)
                waited_on.add(d)
                for k2, v2 in op_know[d].items():
                    if know.get(k2, 0) < v2:
                        know[k2] = v2
                know[sk] = max(know.get(sk, 0), op_sig[d][1])
            sk = ("c", o.chan) if o.is_dma else ("e", o.eng)
            cnt[sk] = cnt.get(sk, 0) + 1
            op_sig[i] = (sk, cnt[sk])
            op_know[i] = dict(know)
            for t in o.reads:
                readers.setdefault(t, []).append(i)
            for t in o.writes:
                last_w[t] = i
                readers[t] = []
        final_deps = set()
        for t in final_wait_tokens:
            w = last_w.get(t)
            if w is not None:
                final_deps.add(w)
                waited_on.add(w)
        inc_val, running = {}, {}
        for i, o in enumerate(ops):
            sk = op_sig[i][0]
            if o.is_dma:
                running[sk] = running.get(sk, 0) + 16
                inc_val[i] = running[sk]
            elif i in waited_on:
                running[sk] = running.get(sk, 0) + 1
                inc_val[i] = running[sk]
        sems = {}
        for sk in running:
            sems[sk] = self.stack.enter_context(nc.semaphore(f"sem{len(sems)}"))
        per_eng = {e: [] for e in ENGS}
        for i, o in enumerate(ops):
            per_eng[o.eng].append(i)
        self.stats = dict(n_ops=len(ops), n_waits=sum(len(o.waits) for o in ops), n_sems=len(sems),
                          per_eng={e: len(v) for e, v in per_eng.items()})

        def emit_engine(ename, eobj, extra_final=False):
            for i in per_eng[ename]:
                o = ops[i]
                for d in o.waits:
                    eobj.wait_ge(sems[op_sig[d][0]], inc_val[d])
                ins = o.fn(eobj)
                if i in inc_val:
                    ins.then_inc(sems[op_sig[i][0]], 16 if o.is_dma else 1)
            if extra_final:
                for d in sorted(final_deps):
                    eobj.wait_ge(sems[op_sig[d][0]], inc_val[d])

        with nc.Block() as block:
            @block.sync
            def _(e):
                emit_engine("sp", e, extra_final=True)

            @block.tensor
            def _(e):
                emit_engine("pe", e)

            @block.scalar
            def _(e):
                emit_engine("act", e)

            @block.vector
            def _(e):
                emit_engine("dve", e)

            @block.gpsimd
            def _(e):
                emit_engine("pool", e)
        self.stack.close()


C_ID, C_LTRI, C_SELOWN, C_SEL0, C_SEL1, C_MASKI, C_MASKS, C_LFULL, C_ONES, C_MTRI = range(10)
NCONST = 10


def make_consts():
    p = np.arange(128)[:, None]
    i = np.arange(128)[None, :]
    same = (p // 64) == (i // 64)
    c = np.zeros((NCONST, 128, 128), np.float32)
    c[C_ID] = (p == i)
    c[C_LTRI] = (p <= i) & same
    c[C_SELOWN] = (p == 64 * (i // 64) + 63)
    c[C_SEL0] = (p == 63) & (i >= 0)
    c[C_SEL1] = (p == 127) & (i >= 0)
    c[C_MASKI] = np.where((i >= p) & same, 0.0, NEG)
    c[C_MASKS] = np.where((i > p) & same, 0.0, NEG)
    c[C_LFULL] = (p <= i)
    c[C_ONES] = 1.0
    c[C_MTRI] = np.where(i >= p, 0.0, NEG)
    return c


class Builder:
    def __init__(self, L, mode="both", dbg=()):
        self.L = L
        self.mode = mode
        self.dbg = set(dbg)
        self.NB = L // 128
        self.TG = min(512, L)
        self.NG = L // self.TG
        self.NBG = self.TG // 128
        self.nc = bass.Bass("TRN2", target_bir_lowering=False)
        self.P = Prog(self.nc)
        self.final_tokens = []
        self.dram = {}
        self.prep = []
        self.prep_done = 0
        self.prep_issued = 0

    def din(self, name, shape, dt=F32):
        t = self.nc.dram_tensor(name, list(shape), dt, kind="ExternalInput").ap()
        self.dram[name] = t
        return t

    def dout(self, name, shape, dt=F32):
        t = self.nc.dram_tensor(name, list(shape), dt, kind="ExternalOutput").ap()
        self.dram[name] = t
        return t

    def dscr(self, name, shape, dt=F32):
        return self.nc.dram_tensor(name, list(shape), dt, kind="Internal").ap()

    def act(self, out, in_, func, r, w, **kw):
        self.P.op("act", lambda e: e.activation(out=out, in_=in_, func=func, **kw), r, w)

    def tt(self, eng, out, in0, in1, op, r, w):
        self.P.op(eng, lambda e: e.tensor_tensor(out=out, in0=in0, in1=in1, op=op), r, w)

    def ts(self, eng, out, in0, s1, op0, r, w, s2=None, op1=None):
        if op1 is None and eng == "pool" and op0 == ALU.mult:
            s2, op1 = 1.0, ALU.mult
        if op1 is None:
            self.P.op(eng, lambda e: e.tensor_scalar(out=out, in0=in0, scalar1=s1, scalar2=None, op0=op0), r, w)
        else:
            self.P.op(eng, lambda e: e.tensor_scalar(out=out, in0=in0, scalar1=s1, scalar2=s2, op0=op0, op1=op1), r, w)

    def stt(self, eng, out, in0, scalar, in1, op0, op1, r, w):
        self.P.op(eng, lambda e: e.scalar_tensor_tensor(out=out, in0=in0, scalar=scalar, in1=in1, op0=op0, op1=op1), r, w)

    def cp(self, eng, out, in_, r, w):
        if eng == "act":
            self.P.op(eng, lambda e: e.activation(out=out, in_=in_, func=AF.Copy), r, w)
        else:
            self.P.op(eng, lambda e: e.tensor_copy(out=out, in_=in_), r, w)

    def mm(self, out, lhsT, rhs, start, stop, r, w):
        self.P.op("pe", lambda e: e.matmul(out=out, lhsT=lhsT, rhs=rhs, start=start, stop=stop), r, w)

    def tr(self, out, in_, r, w):
        idb = self.idb
        self.P.op("pe", lambda e: e.transpose(out=out, in_=in_, identity=idb[:]), tuple(r) + ("idb",), w)

    def dma(self, out, in_, chan, r, w, eng="sp"):
        self.P.dma(eng, lambda e: e.dma_start(out=out, in_=in_), chan, r, w)

    def memset(self, eng, ap, val, w):
        self.P.op(eng, lambda e: e.memset(ap, val), (), w)

    def bank(self, b, n=1):
        return self.PS[:, b * 512:(b + n) * 512]

    def bank_bf(self, b):
        return self.PS[:, b * 512:(b + 1) * 512].bitcast(BF16)

    def debug_out(self, name, sb_ap, shape, token, dt=F32):
        if name not in self.dbg:
            return
        d = self.dout("dbg_" + name, shape, dt)
        self.dma(d, sb_ap, "dbg_" + name, [token], ["dbgo_" + name])
        self.final_tokens.append("dbgo_" + name)

    def arena_reset(self):
        self.a_off = 0

    def al(self, name, shape, dtype):
        shape = list(shape)
        n = 1
        for d_ in shape[1:]:
            n *= d_
        words = n if dtype == F32 else (n + 1) // 2
        words = (words + 1) // 2 * 2
        assert self.a_off + words <= self.ARENA_W, (name, self.a_off, words)
        ap = self.arena[0:shape[0], self.a_off:self.a_off + words]
        self.a_off += words
        if dtype != F32:
            ap = ap.bitcast(dtype)
        ap = ap[:, 0:n]
        if len(shape) == 3:
            ap = ap.rearrange("p (a b) -> p a b", a=shape[1])
        elif len(shape) == 4:
            ap = ap.rearrange("p (a b c) -> p a b c", a=shape[1], b=shape[2])
        return ap

    def join(self, tokens, col):
        jn = self.jn
        self.P.op("pool", lambda e: e.memset(jn[0:1, col:col + 1], 0.0), tuple(tokens), tuple(tokens))

    def prep_add(self, fn_in, fn_rest):
        self.prep.append((fn_in, fn_rest))

    def prep_emit(self, n):
        for _ in range(n):
            p = self.prep_done
            if p >= len(self.prep):
                return
            if self.prep_issued <= p:
                self.prep[p][0](p % 2)
                self.prep_issued = p + 1
            if p + 1 < len(self.prep) and self.prep_issued <= p + 1:
                self.prep[p + 1][0]((p + 1) % 2)
                self.prep_issued = p + 2
            self.prep[p][1](p % 2)
            self.prep_done += 1

    def prep_emit_until(self, idx):
        while self.prep_done <= idx and self.prep_done < len(self.prep):
            self.prep_emit(1)

    def barrier(self, tag):
        bar = self.bar
        engs = ["act", "dve", "pool", "pe", "sp"]
        for i, e_ in enumerate(engs):
            n0 = len(self.P.ops)
            self._dummy(e_, i, [], [f"bar_{tag}_{e_}"])
            self.P.ops[n0].all_dma = BAR_DMA
        for i, e_ in enumerate(engs):
            self._dummy(e_, i, [f"bar_{tag}_{x}" for x in engs], [f"bar2_{tag}_{e_}"])

    def _dummy(self, eng, i, r, w):
        bar = self.bar
        if eng == "act":
            self.P.op("act", lambda e: e.activation(out=bar[0:1, i:i + 1], in_=bar[0:1, 8:9], func=AF.Copy), r, w)
        elif eng in ("dve", "pool"):
            self.P.op(eng, lambda e: e.memset(bar[0:1, i:i + 1], 0.0), r, w)
        elif eng == "pe":
            PS = self.PS
            self.P.op("pe", lambda e: e.matmul(out=PS[0:1, 4095:4096], lhsT=self.idb[:, 0:1], rhs=self.idb[:, 0:1], start=True, stop=True),
                      tuple(r) + ("idb",), tuple(w) + ("ps7",))
        else:
            self.P.dma("sp", lambda e: e.dma_start(out=self.bar_d[0:1, 0:8], in_=bar[0:1, 8:16]), "bar", r, w)

    def setup_common(self):
        P = self.P
        self.consts_d = self.din("consts", [NCONST, 128, 128])
        self.cst = P.sbuf("cst", [128, NCONST, 128], F32)
        self.idb = P.sbuf("idb", [128, 128], BF16)
        self.onesb = P.sbuf("onesb", [128, 128], BF16)
        self.PS = P.psum("PS", [128, 4096], F32)
        self.bar = P.sbuf("bar", [1, 16], F32)
        self.jn = P.sbuf("jn", [1, 16], F32)
        self.bar_d = self.dscr("bar_d", [1, 8])
        self.memset("pool", self.bar[:], 0.0, ["bar_init"])
        self.stg = [P.sbuf(f"stg{i}", [128, 1040], F32) for i in range(2)]
        self.ARENA_W = 48600
        self.arena = P.sbuf("arena", [128, self.ARENA_W], F32)
        self.arena_reset()
        self.dma(self.cst[:], self.consts_d.rearrange("n p f -> p n f"), "cst", [], ["cst"])
        self.cp("pool", self.idb[:], self.cst[:, C_ID, :], ["cst"], ["idb"])
        self.cp("pool", self.onesb[:], self.cst[:, C_ONES, :], ["cst"], ["onesb"])

    def layer1(self, x_d, h1_d):
        P = self.P
        L, TG, NG, NBG = self.L, self.TG, self.NG, self.NBG
        cst = self.cst
        idb = self.idb
        w1qkv_d = self.din("w1qkv", [24, 128, 8, 128])
        w1z_d = self.din("w1z", [128, 8, 1040])
        gpre_d = self.din("gpre", [128, 8])
        cw_d = self.din("cw", [128, 24, 4])
        alog_d = self.din("alog_b", [128, 8])
        dtb_d = self.din("dtb_b", [128, 8])
        gon_d = self.din("gonorm", [128, 1])
        wout_d = self.din("wout1", [128, 8, 1024])
        gpost_d = self.din("gpost1", [1, 1024])
        W1s = self.dscr("W1s", [24, 128, 1024], BF16)
        gcs = self.dscr("gcs", [NG, NBG * 8, 128], F32)
        gpre = self.al("gpre", [128, 8], F32)
        cw = self.al("cw", [128, 24, 4], F32)
        convd = self.al("convd", [128, 24, 4, 128], BF16)
        negA = self.al("negA", [128, 8], F32)
        dtb = self.al("dtb", [128, 8], F32)
        gon = self.al("gon", [128, 1], F32)
        gpost = self.al("gpost", [128, 1024], F32)
        Wout = self.al("Wout", [128, 8, 1024], BF16)
        W1z = self.al("W1z", [128, 8, 1040], BF16)
        stg = self.stg
        wbuf = [self.al(f"wbuf{i}", [128, 8, 128], BF16) for i in range(3)]
        xt = [self.al(f"xt{i}", [128, 1024], F32) for i in range(2)]
        junk = self.al("junk", [128, 1024], BF16)
        xn2 = [self.al(f"xn{i}", [128, 1024], BF16) for i in range(2)]
        xnT = self.al("xnT", [128, 8, TG], BF16)
        sm = self.al("sm", [128, 64], F32)
        pc = [self.al(f"pc{i}", [128, TG + 3], BF16) for i in range(3)]
        halo = self.al("halo", [128, 24, 3], BF16)
        qkvT = self.al("qkvT", [128, 24, TG], BF16)
        sq = [self.al(f"sq{i}", [128, TG], BF16) for i in range(2)]
        zs = self.al("zs", [128, NBG, 1024], BF16)
        ab = self.al("ab", [128, NBG, 16], F32)
        def sc(name, n=8):
            return self.al(name, [128, NBG, n], F32)
        g_t, lnb, beta, gc, glown = sc("g_t"), sc("lnb"), sc("beta"), sc("gc"), sc("glown")
        lnss = sc("lnss", 16)
        rq, rk, nk, lnrk = sc("rq"), sc("rk"), sc("nk"), sc("lnrk")
        sc_kg, sc_kd, sc_vnew, sc_o, sc_oi, bias1, bias2 = (sc("sc_kg"), sc("sc_kd"), sc("sc_vnew"), sc("sc_o"),
                                                              sc("sc_oi"), sc("bias1"), sc("bias2"))
        tmp8 = sc("tmp8")
        dec = self.al("dec", [128, NBG, 2, 8], F32)
        gcT = self.al("gcT", [NBG * 8, 128], F32)
        kg = self.al("kg", [128, 8, 128], BF16)
        kd = self.al("kd", [128, 8, 128], BF16)
        vn = self.al("vn", [128, 8, 128], BF16)
        attnT = self.al("attnT", [128, 8, 128], BF16)
        nwT = self.al("nwT", [128, 8, 128], BF16)
        vnew = self.al("vnew", [128, 8, 128], BF16)
        X1 = self.al("X1", [128, 8, 128], F32)
        X2 = self.al("X2", [128, 8, 128], F32)
        Pm = [self.al(f"Pm{i}", [128, 8, 128], BF16) for i in range(2)]
        Qm = [self.al(f"Qm{i}", [128, 8, 128], BF16) for i in range(2)]
        TTm = [self.al(f"TTm{i}", [128, 8, 128], BF16) for i in range(2)]
        S32 = self.al("S32", [128, 8, 128], F32)
        Sd = self.al("Sd", [128, 8, 128], F32)
        Sbf = self.al("Sbf", [128, 8, 128], BF16)
        t1 = self.al("t1", [128, 8, 128], F32)
        t2 = self.al("t2", [128, 8, 128], F32)
        og = self.al("og", [128, 8, 128], BF16)
        ogT = self.al("ogT", [128, 8, 128], BF16)
        sso = self.al("sso", [128, 8], F32)
        so = self.al("so", [128, 8], F32)
        xr = self.al("xr", [128, 1024], F32)

        def bc8(ap2):
            return ap2.unsqueeze(2).to_broadcast([128, 8, 128])

        self.dma(gpre[:], gpre_d, "iniA", [], ["gpre"])
        self.dma(cw[:], cw_d, "iniA", [], ["cw"])
        self.dma(negA[:], alog_d, "iniA", [], ["negA"])
        self.dma(dtb[:], dtb_d, "iniA", [], ["dtb"])
        self.dma(gon[:], gon_d, "iniA", [], ["gon"])
        self.dma(gpost[:], gpost_d.partition_broadcast(128), "iniA", [], ["gpost"])
        self.join(["gpre", "cw", "negA", "dtb", "gon", "gpost"], 9)
        self.act(negA[:], negA[:], AF.Exp, ["negA"], ["negA"])
        self.ts("dve", negA[:], negA[:], -1.0, ALU.mult, ["negA"], ["negA"])
        for k in range(4):
            self.tt("pool", convd[:, :, k, :], idb[:].unsqueeze(1).to_broadcast([128, 24, 128]),
                    cw[:, :, k].unsqueeze(2).to_broadcast([128, 24, 128]), ALU.mult,
                    ["idb", "cw"], ["convd"])
        for c in range(24):
            def fin(slot, c=c):
                sv = stg[slot][:, 0:1024].rearrange("p (k n) -> p k n", k=8)
                self.dma(sv, w1qkv_d[c], f"stg{slot}", [], [f"stg{slot}"], eng="pool")

            def frest(slot, c=c):
                sv = stg[slot][:, 0:1024].rearrange("p (k n) -> p k n", k=8)
                wb = wbuf[c % 3]
                self.tt("pool", wb[:], sv, gpre[:, :].unsqueeze(2).to_broadcast([128, 8, 128]), ALU.mult,
                        [f"stg{slot}", "gpre"], [f"wbuf{c % 3}"])
                self.dma(W1s[c].rearrange("p (k n) -> p k n", k=8), wb[:], f"w1s{c % 3}", [f"wbuf{c % 3}"], [f"W1s{c}"], eng="pool")
            self.prep_add(fin, frest)
        for kc in range(8):
            def fin(slot, kc=kc):
                self.dma(stg[slot][:, 0:1040], w1z_d[:, kc, :], f"stg{slot}", [], [f"stg{slot}"], eng="pool")

            def frest(slot, kc=kc):
                self.ts("pool", W1z[:, kc, :], stg[slot][:, 0:1040], gpre[:, kc:kc + 1], ALU.mult, [f"stg{slot}", "gpre"], ["W1z"])
            self.prep_add(fin, frest)
        for hh in range(8):
            def fin(slot, hh=hh):
                self.dma(stg[slot][:, 0:1024], wout_d[:, hh, :], f"stg{slot}", [], [f"stg{slot}"], eng="pool")

            def frest(slot, hh=hh):
                self.ts("pool", Wout[:, hh, :], stg[slot][:, 0:1024], gon[:, 0:1], ALU.mult, [f"stg{slot}", "gon"], ["Wout"])
            self.prep_add(fin, frest)
        self.l1_prep_end = len(self.prep)
        if self.mode == "both":
            self.layer2_prep()
        self.memset("pool", S32[:], 0.0, ["S32_0", "S32_1"])
        self.memset("pool", Sbf[:], 0.0, ["Sbf0", "Sbf1"])
        self.memset("pool", halo[:], 0.0, [f"halo{c}" for c in range(24)])

        PS = self.PS
        sqj = self.al("sqj", [128, 8, 128], BF16)

        def make_F(bi, t0):
            pOI = self.bank(2, 2).rearrange("p (h i) -> p h i", h=8)
            pOA = self.bank(4, 2).rearrange("p (h i) -> p h i", h=8)
            yt = t2[:].rearrange("p h d -> p (h d)")
            r0 = t0 + bi * 128

            def f0():
                self.tt("dve", t1[:], pOI, bc8(sc_oi[:, bi, :]), ALU.mult, ["ps2", "ps3", "sc_oi"], ["t1"])
                self.tt("dve", t2[:], pOA, bc8(sc_o[:, bi, :]), ALU.mult, ["ps4", "ps5", "sc_o"], ["t2"])
                self.dma(xr[:], x_d[r0:r0 + 128, :], "xr", [], ["xr"])

            def f1():
                self.tt("pool", t1[:], t1[:], t2[:], ALU.add, ["t1", "t2"], ["t1"])
                for hh in range(8):
                    self.act(sqj[:, hh, :], t1[:, hh, :], AF.Square, ["t1"], [f"sqj{hh}", f"sso{hh}"], accum_out=sso[:, hh:hh + 1])

            def f2():
                self.act(so[:], sso[:], AF.Ln, [f"sso{q}" for q in range(8)], ["so"], scale=1.0 / 128, bias=EPS)
                self.act(so[:], so[:], AF.Exp, ["so"], ["so"], scale=-0.5)
                self.tt("pool", t2[:], t1[:], zs[:, bi, :].rearrange("p (h d) -> p h d", h=8), ALU.mult, ["t1", f"zs{bi}"], ["t2"])
                self.tt("dve", og[:], t2[:], bc8(so[:, :]), ALU.mult, ["t2", "so"], ["og"])

            def f3():
                pO = self.bank_bf(6).rearrange("p (h d) -> p h d", h=8)
                for hh in range(8):
                    self.tr(pO[:, hh, :], og[:, hh, :], ["og"], ["ps6"])
                self.cp("act", ogT[:], pO, ["ps6"], ["ogT"])

            def f4():
                for half in range(2):
                    for hh in range(8):
                        self.mm(self.bank(6 + half), ogT[:, hh, :], Wout[:, hh, half * 512:(half + 1) * 512], hh == 0, hh == 7,
                                ["ogT", "Wout"], [f"ps{6 + half}"])

            def f5():
                py = self.bank(6, 2)
                self.act(t1[:].rearrange("p h d -> p (h d)"), py, AF.Square, ["ps6", "ps7"], ["t1", "sm4"], accum_out=sm[:, 4:5])
                self.act(sm[:, 5:6], sm[:, 4:5], AF.Ln, ["sm4"], ["sm5"], scale=1.0 / D, bias=EPS)
                self.act(sm[:, 6:7], sm[:, 5:6], AF.Exp, ["sm5"], ["sm6"], scale=-0.5)
                self.stt("dve", yt, py, sm[:, 6:7], gpost[:], ALU.mult, ALU.mult, ["ps6", "ps7", "sm6", "gpost"], ["t2"])

            def f6():
                self.tt("pool", xr[:], yt, xr[:], ALU.add, ["t2", "xr"], ["xr"])
                self.dma(h1_d[r0:r0 + 128, :], xr[:], "h1st", ["xr"], ["h1_dram"], eng="pool")
            return [f0, f1, f2, f3, f4, f5, f6]

        pendF = []
        for g in range(NG):
            t0 = g * TG
            def a1(bi, t0=t0):
                xs = xt[bi % 2]
                xtok = f"xt{bi % 2}"
                so_ = 8 + 3 * (bi % 2)
                r0 = t0 + bi * 128
                self.dma(xs[:], x_d[r0:r0 + 128, :], xtok, [], [xtok])
                self.act(junk[:], xs[:], AF.Square, [xtok], ["junk", f"sm{so_}"], accum_out=sm[:, so_:so_ + 1])
                self.act(sm[:, so_ + 1:so_ + 2], sm[:, so_:so_ + 1], AF.Ln, [f"sm{so_}"], [f"sm{so_ + 1}"], scale=1.0 / D, bias=EPS)
                self.act(sm[:, so_ + 2:so_ + 3], sm[:, so_ + 1:so_ + 2], AF.Exp, [f"sm{so_ + 1}"], [f"sm{so_ + 2}"], scale=-0.5)
                self.ts("dve", xn2[bi % 2][:], xs[:], sm[:, so_ + 2:so_ + 3], ALU.mult, [xtok, f"sm{so_ + 2}"], [f"xn{bi % 2}"])

            def a2(bi):
                pT = self.bank_bf(bi % 2).rearrange("p (k i) -> p k i", k=8)
                for kc in range(8):
                    self.tr(pT[:, kc, :], xn2[bi % 2][:, kc * 128:(kc + 1) * 128], [f"xn{bi % 2}"], [f"ps{bi % 2}"])
                self.cp("act" if bi % 2 == 0 else "dve", xnT[:, :, bi * 128:(bi + 1) * 128], pT, [f"ps{bi % 2}"], ["xnT"])

            a1(0)
            for bi in range(NBG):
                if bi + 1 < NBG:
                    a1(bi + 1)
                a2(bi)
            def proj_part(c, g=g):
                wb = wbuf[c % 3]
                wtok = f"wbuf{c % 3}"
                if g == 0:
                    self.prep_emit_until(c)
                else:
                    self.dma(wb[:], W1s[c].rearrange("p (k n) -> p k n", k=8), f"wld{c % 3}", [f"W1s{c}"], [wtok])
                pb = c % 2
                for kc in range(8):
                    self.mm(self.bank(pb), wb[:, kc, :], xnT[:, kc, :], kc == 0, kc == 7, [wtok, "xnT"], [f"ps{pb}"])
                pcb = pc[c % 3]
                ptok = f"pc{c % 3}"
                self.cp("act" if c % 2 == 0 else "dve", pcb[:, 3:3 + TG], self.bank(pb), [f"ps{pb}"], [ptok])
                self.cp("pool", pcb[:, 0:3], halo[:, c, :], [f"halo{c}"], [ptok])
                self.cp("pool", halo[:, c, :], pcb[:, TG:TG + 3], [ptok], [f"halo{c}"])

            def conv_part(c):
                pcb = pc[c % 3]
                ptok = f"pc{c % 3}"
                cb = 2 + (c % 2)
                for k in range(4):
                    self.mm(self.bank(cb), convd[:, c, k, :], pcb[:, k:k + TG], k == 0, k == 3, [ptok, "convd"], [f"ps{cb}"])
                self.act(qkvT[:, c, :], self.bank(cb), AF.Silu, [f"ps{cb}"], [f"qkvT{c}"])

            for c in range(25):
                if c < 24:
                    proj_part(c)
                if c >= 1:
                    conv_part(c - 1)
            if g == 0:
                self.prep_emit_until(self.l1_prep_end - 9)
            for c in range(16):
                sqb = sq[c % 2]
                self.act(sqb[:], qkvT[:, c, :], AF.Square, [f"qkvT{c}"], [f"sq{c % 2}"])
                for bi in range(NBG):
                    col = 7 * 512 + bi * 16 + c
                    self.mm(PS[:, col:col + 1], sqb[:, bi * 128:(bi + 1) * 128], self.onesb[:, 0:1], True, True,
                            [f"sq{c % 2}", "onesb"], ["ps7"])
            for bi in range(NBG):
                for half in range(2):
                    for kc in range(8):
                        self.mm(self.bank(4 + half), xnT[:, kc, bi * 128:(bi + 1) * 128], W1z[:, kc, half * 512:(half + 1) * 512],
                                kc == 0, kc == 7, ["xnT", "W1z"], [f"ps{4 + half}"])
                for kc in range(8):
                    self.mm(PS[:, 6 * 512:6 * 512 + 16], xnT[:, kc, bi * 128:(bi + 1) * 128], W1z[:, kc, 1024:1040],
                            kc == 0, kc == 7, ["xnT", "W1z"], ["ps6"])
                self.act(zs[:, bi, 0:512], self.bank(4), AF.Silu, ["ps4"], [f"zs{bi}"])
                self.act(zs[:, bi, 512:1024], self.bank(5), AF.Silu, ["ps5"], [f"zs{bi}"])
                self.cp("dve", ab[:, bi, :], PS[:, 6 * 512:6 * 512 + 16], ["ps6"], ["ab"])
            a_ap, b_ap = ab[:, :, 0:8], ab[:, :, 8:16]
            self.tt("dve", tmp8[:], a_ap, dtb[:, :].unsqueeze(1).to_broadcast([128, NBG, 8]), ALU.add, ["ab", "dtb"], ["tmp8"])
            self.act(tmp8[:], tmp8[:], AF.Exp, ["tmp8"], ["tmp8"])
            self.act(tmp8[:], tmp8[:], AF.Ln, ["tmp8"], ["tmp8"], bias=1.0)
            self.tt("dve", g_t[:], tmp8[:], negA[:, :].unsqueeze(1).to_broadcast([128, NBG, 8]), ALU.mult, ["tmp8", "negA"], ["g_t"])
            self.act(lnb[:], b_ap, AF.Exp, ["ab"], ["lnb"], scale=-1.0)
            self.act(lnb[:], lnb[:], AF.Ln, ["lnb"], ["lnb"], bias=1.0)
            self.act(beta[:], lnb[:], AF.Exp, ["lnb"], ["beta"], scale=-1.0)
            gflat = g_t[:].rearrange("p b h -> p (b h)")
            n8 = NBG * 8
            o6 = 6 * 512
            self.mm(PS[:, o6 + 64:o6 + 64 + n8], cst[:, C_LTRI, :], gflat, True, True, ["g_t", "cst"], ["ps6"])
            self.cp("dve", gc[:].rearrange("p b h -> p (b h)"), PS[:, o6 + 64:o6 + 64 + n8], ["ps6"], ["gc"])
            gcflat = gc[:].rearrange("p b h -> p (b h)")
            self.mm(PS[:, o6 + 128:o6 + 128 + n8], cst[:, C_SELOWN, :], gcflat, True, True, ["gc", "cst"], ["ps6"])
            self.mm(PS[:, o6 + 192:o6 + 192 + n8], cst[:, C_SEL0, :], gcflat, True, True, ["gc", "cst"], ["ps6"])
            self.mm(PS[:, o6 + 256:o6 + 256 + n8], cst[:, C_SEL1, :], gcflat, True, True, ["gc", "cst"], ["ps6"])
            self.mm(PS[0:n8, o6 + 320:o6 + 448], gcflat, cst[:, C_ID, :], True, True, ["gc", "cst"], ["ps6"])
            self.cp("dve", glown[:].rearrange("p b h -> p (b h)"), PS[:, o6 + 128:o6 + 128 + n8], ["ps6"], ["glown"])
            for c2 in range(2):
                off = o6 + 192 + 64 * c2
                self.act(dec[:, :, c2, :], PS[:, off:off + n8].rearrange("p (b h) -> p b h", h=8), AF.Exp, ["ps6"], ["dec"])
            self.cp("act", gcT[:], PS[0:n8, o6 + 320:o6 + 448], ["ps6"], ["gcT"])
            self.dma(gcs[g], gcT[:], "gcs", ["gcT"], [f"gcs{g}"])
            self.act(lnss[:], PS[:, 7 * 512:7 * 512 + NBG * 16].rearrange("p (b c) -> p b c", c=16), AF.Ln, ["ps7"], ["lnss"], bias=EPS)
            self.act(rq[:], lnss[:, :, 0:8], AF.Exp, ["lnss"], ["rq"], scale=-0.5)
            self.act(rk[:], lnss[:, :, 8:16], AF.Exp, ["lnss"], ["rk"], scale=-0.5)
            self.act(nk[:], lnss[:, :, 8:16], AF.Exp, ["lnss"], ["nk"], scale=0.5)
            self.ts("dve", lnrk[:], lnss[:, :, 8:16], -0.5, ALU.mult, ["lnss"], ["lnrk"])
            self.act(sc_kg[:], gc[:], AF.Exp, ["gc"], ["sc_kg"])
            self.tt("dve", tmp8[:], glown[:], gc[:], ALU.subtract, ["glown", "gc"], ["tmp8"])
            self.tt("dve", tmp8[:], tmp8[:], lnrk[:], ALU.add, ["tmp8", "lnrk"], ["tmp8"])
            self.act(sc_kd[:], tmp8[:], AF.Exp, ["tmp8"], ["sc_kd"])
            self.tt("dve", sc_vnew[:], beta[:], rk[:], ALU.mult, ["beta", "rk"], ["sc_vnew"])
            self.ts("dve", sc_o[:], rq[:], 128.0 ** -0.5, ALU.mult, ["rq"], ["sc_o"])
            self.tt("dve", sc_oi[:], sc_o[:], sc_kg[:], ALU.mult, ["sc_o", "sc_kg"], ["sc_oi"])
            self.tt("dve", bias1[:], lnrk[:], gc[:], ALU.subtract, ["lnrk", "gc"], ["bias1"])
            self.tt("dve", bias2[:], bias1[:], lnrk[:], ALU.add, ["bias1", "lnrk"], ["bias2"])
            self.tt("dve", bias2[:], bias2[:], lnb[:], ALU.subtract, ["bias2", "lnb"], ["bias2"])
            if g == 0:
                self.prep_emit_until(self.l1_prep_end - 1)
            for bi in range(NBG):
                blk = slice(bi * 128, (bi + 1) * 128)
                if not (g == 0 and bi == 0):
                    self.prep_emit(3)
                if pendF:
                    pendF.pop(0)()
                pk = self.bank_bf(0).rearrange("p (h d) -> p h d", h=8)
                for hh in range(8):
                    self.tr(pk[:, hh, :], qkvT[:, 8 + hh, blk], [f"qkvT{8 + hh}"], ["ps0"])
                self.tt("dve", kg[:], pk, bc8(sc_kg[:, bi, :]), ALU.mult, ["ps0", "sc_kg"], ["kg"])
                self.tt("dve", kd[:], pk, bc8(sc_kd[:, bi, :]), ALU.mult, ["ps0", "sc_kd"], ["kd"])
                pv = self.bank_bf(1).rearrange("p (h d) -> p h d", h=8)
                for hh in range(8):
                    self.tr(pv[:, hh, :], qkvT[:, 16 + hh, blk], [f"qkvT{16 + hh}"], ["ps1"])
                self.tt("dve", vn[:], pv, bc8(nk[:, bi, :]), ALU.mult, ["ps1", "nk"], ["vn"])
                self.dma(X2[:].rearrange("p h i -> p (h i)"),
                         gcs[g, bi * 8:(bi + 1) * 8, :].rearrange("(o h) i -> o (h i)", o=1).partition_broadcast(128),
                         "e2b", [f"gcs{g}"], [f"X2_{q}" for q in range(8)])
                self.tt("pool", X1[:], X2[:], cst[:, C_MASKI, :].unsqueeze(1).to_broadcast([128, 8, 128]), ALU.add, [f"X2_{q}" for q in range(8)] + ["cst"], [f"X1_{q}" for q in range(8)])
                self.tt("pool", X2[:], X2[:], cst[:, C_MASKS, :].unsqueeze(1).to_broadcast([128, 8, 128]), ALU.add, [f"X2_{q}" for q in range(8)] + ["cst"], [f"X2_{q}" for q in range(8)])
                for hh in range(8):
                    self.act(X1[:, hh, :], X1[:, hh, :], AF.Exp, [f"X1_{hh}", "bias1"], [f"X1_{hh}"], bias=bias1[:, bi, hh:hh + 1])
                for hh in range(8):
                    self.act(X2[:, hh, :], X2[:, hh, :], AF.Exp, [f"X2_{hh}", "bias2"], [f"X2_{hh}"], bias=bias2[:, bi, hh:hh + 1])
                pG = self.bank(2, 2).rearrange("p (h i) -> p h i", h=8)
                pQK = self.bank(4, 2).rearrange("p (h i) -> p h i", h=8)
                for hh in range(8):
                    self.mm(pG[:, hh, :], qkvT[:, 8 + hh, blk], qkvT[:, 8 + hh, blk], True, True, [f"qkvT{8 + hh}"], [f"ps{2 + hh // 4}"])
                for hh in range(8):
                    self.mm(pQK[:, hh, :], qkvT[:, 8 + hh, blk], qkvT[:, hh, blk], True, True, [f"qkvT{8 + hh}", f"qkvT{hh}"], [f"ps{4 + hh // 4}"])
                Q0, P0, T0 = Qm[0], Pm[0], TTm[0]
                hs_ = [slice(0, 4), slice(4, 8)]
                for hf in range(2):
                    self.stt("dve", Q0[:, hs_[hf], :], pG[:, hs_[hf], :], -1.0, X2[:, hs_[hf], :], ALU.mult, ALU.mult,
                             [f"ps{2 + hf}"] + [f"X2_{q}" for q in range(4 * hf, 4 * hf + 4)], [f"Qm0_{hf}"])
                self.tt("dve", attnT[:], pQK, X1[:], ALU.mult, ["ps4", "ps5"] + [f"X1_{q}" for q in range(8)], ["attnT"])
                pP = self.bank_bf(0).rearrange("p (h d) -> p h d", h=8)
                for hh in range(8):
                    self.tr(pP[:, hh, :], Q0[:, hh, :], [f"Qm0_{hh // 4}"], ["ps0"])
                for hf in range(2):
                    self.cp("act", P0[:, hs_[hf], :], pP[:, hs_[hf], :], ["ps0"], [f"Pm0_{hf}"])
                    self.tt("pool", T0[:, hs_[hf], :], Q0[:, hs_[hf], :], idb[:].unsqueeze(1).to_broadcast([128, 4, 128]), ALU.add,
                            [f"Qm0_{hf}", "idb"], [f"TTm0_{hf}"])
                for k in range(1, 6):
                    pi, ci = (k - 1) % 2, k % 2
                    Pp, Qp, Tp = Pm[pi], Qm[pi], TTm[pi]
                    Pc, Qc, Tc = Pm[ci], Qm[ci], TTm[ci]
                    pPk = self.bank(0, 2).rearrange("p (h i) -> p h i", h=8)
                    pQk = self.bank(2, 2).rearrange("p (h i) -> p h i", h=8)
                    pTk = self.bank(4, 2).rearrange("p (h i) -> p h i", h=8)
                    for hf in range(2):
                        for hh in range(4 * hf, 4 * hf + 4):
                            self.mm(pPk[:, hh, :], Qp[:, hh, :], Pp[:, hh, :], True, True, [f"Qm{pi}_{hf}", f"Pm{pi}_{hf}"], [f"ps{hf}"])
                        if k < 5:
                            for hh in range(4 * hf, 4 * hf + 4):
                                self.mm(pQk[:, hh, :], Pp[:, hh, :], Qp[:, hh, :], True, True, [f"Qm{pi}_{hf}", f"Pm{pi}_{hf}"], [f"ps{2 + hf}"])
                    for hf in range(2):
                        self.cp("act", Pc[:, hs_[hf], :], pPk[:, hs_[hf], :], [f"ps{hf}"], [f"Pm{ci}_{hf}"])
                        if k < 5:
                            self.cp("dve", Qc[:, hs_[hf], :], pQk[:, hs_[hf], :], [f"ps{2 + hf}"], [f"Qm{ci}_{hf}"])
                    for hf in range(2):
                        for hh in range(4 * hf, 4 * hf + 4):
                            self.mm(pTk[:, hh, :], idb[:], Tp[:, hh, :], True, False, ["idb", f"TTm{pi}_{hf}"], [f"ps{4 + hf}"])
                            self.mm(pTk[:, hh, :], Pc[:, hh, :], Tp[:, hh, :], False, True, [f"Pm{ci}_{hf}", f"TTm{pi}_{hf}"], [f"ps{4 + hf}"])
                    for hf in range(2):
                        self.cp("dve" if hf == 0 else "act", Tc[:, hs_[hf], :], pTk[:, hs_[hf], :], [f"ps{4 + hf}"], [f"TTm{ci}_{hf}"])
                    if pendF:
                        pendF.pop(0)()
                while pendF:
                    pendF.pop(0)()
                TT = TTm[1]
                pW = self.bank(6, 2).rearrange("p (h i) -> p h i", h=8)
                for hh in range(8):
                    self.mm(pW[:, hh, :], kg[:, hh, :], TT[:, hh, :], True, True, ["kg", f"TTm1_{hh // 4}"], [f"ps{6 + hh // 4}"])
                self.P.op("act", lambda e, o=nwT[:], i=pW: e.mul(out=o, in_=i, mul=-1.0), ["ps6", "ps7"], ["nwT"])
                pV = self.bank(0, 2).rearrange("p (h i) -> p h i", h=8)
                pOI = self.bank(2, 2).rearrange("p (h i) -> p h i", h=8)
                pOA = self.bank(4, 2).rearrange("p (h i) -> p h i", h=8)
                pDS = self.bank(6, 2).rearrange("p (h i) -> p h i", h=8)
                for c2 in range(2):
                    r = slice(c2 * 64, (c2 + 1) * 64)
                    tokr = slice(bi * 128 + c2 * 64, bi * 128 + (c2 + 1) * 64)
                    for hf in range(2):
                        for hh in range(4 * hf, 4 * hf + 4):
                            self.mm(pV[r, hh, :], TT[r, hh, r], vn[r, hh, :], True, False, [f"TTm1_{hf}", "vn"], [f"ps{hf}"])
                            self.mm(pV[r, hh, :], nwT[:, hh, r], Sbf[:, hh, :], False, True, ["nwT", f"Sbf{hf}"], [f"ps{hf}"])
                    for hf in range(2):
                        hsl = slice(4 * hf, 4 * hf + 4)
                        self.tt("dve", vnew[r, hsl, :], pV[r, hsl, :], sc_vnew[r, bi, hsl].unsqueeze(2).to_broadcast([64, 4, 128]), ALU.mult,
                                [f"ps{hf}", "sc_vnew"], [f"vnew{hf}"])
                        self.tt("pool", Sd[:, hsl, :], S32[:, hsl, :], dec[:, bi, c2, hsl].unsqueeze(2).to_broadcast([128, 4, 128]), ALU.mult,
                                [f"S32_{hf}", "dec"], [f"Sd{hf}"])
                    for hf in range(2):
                        for hh in range(4 * hf, 4 * hf + 4):
                            self.mm(pDS[:, hh, :], kd[r, hh, :], vnew[r, hh, :], True, True, ["kd", f"vnew{hf}"], [f"ps{6 + hf}"])
                    for hf in range(2):
                        for hh in range(4 * hf, 4 * hf + 4):
                            self.mm(pOI[r, hh, :], qkvT[:, hh, tokr], Sbf[:, hh, :], True, True, [f"qkvT{hh}", f"Sbf{hf}"], [f"ps{2 + hf}"])
                    for hf in range(2):
                        hsl = slice(4 * hf, 4 * hf + 4)
                        self.tt("dve", S32[:, hsl, :], Sd[:, hsl, :], pDS[:, hsl, :], ALU.add, [f"Sd{hf}", f"ps{6 + hf}"], [f"S32_{hf}"])
                        self.cp("act", Sbf[:, hsl, :], S32[:, hsl, :], [f"S32_{hf}"], [f"Sbf{hf}"])
                    for hf in range(2):
                        for hh in range(4 * hf, 4 * hf + 4):
                            self.mm(pOA[r, hh, :], attnT[r, hh, r], vnew[r, hh, :], True, True, ["attnT", f"vnew{hf}"], [f"ps{4 + hf}"])
                pendF.extend(make_F(bi, t0))
            while pendF:
                pendF.pop(0)()
        return "h1_dram"


    def layer2_prep(self):
        P = self.P
        stg = self.stg
        w2_d = self.din("w2", [8, 128, 8, 513])
        kvn_d = self.din("kvn", [128, 8])
        fpn_d = self.din("fpn", [128, 8])
        wout2_d = self.din("wout2", [128, 8, 1024])
        self.gk_d = self.din("gk", [128, 1])
        self.gq_d = self.din("gq", [128, 1])
        self.fb_d = self.din("fb_b", [128, 8])
        self.gpost2_d = self.din("gpost2", [1, 1024])
        self.W2s = self.dscr("W2s", [8, 128, 8 * 513], BF16)
        self.Wout2s = self.dscr("Wout2s", [128, 8 * 1024], BF16)
        kvn = P.sbuf("kvn", [128, 8], F32)
        fpn = P.sbuf("fpn", [128, 8], F32)
        wst = [P.sbuf(f"wst{i}", [128, 1024], BF16) for i in range(2)]
        self.dma(kvn[:], kvn_d, "iniK", [], ["kvn"])
        self.dma(fpn[:], fpn_d, "iniK", [], ["fpn"])
        self.join(["kvn", "fpn"], 10)
        for h in range(8):
            for kc in range(8):
                def fin(slot, h=h, kc=kc):
                    self.dma(stg[slot][:, 0:513], w2_d[h, :, kc, :], f"stg{slot}", [], [f"stg{slot}"], eng="pool")

                def frest(slot, h=h, kc=kc):
                    sg, wb = stg[slot], wst[slot]
                    stok, wtok = f"stg{slot}", f"wst{slot}"
                    self.ts("pool", wb[:, 0:128], sg[:, 0:128], kvn[:, kc:kc + 1], ALU.mult, [stok, "kvn"], [wtok])
                    self.ts("pool", wb[:, 128:384], sg[:, 128:384], fpn[:, kc:kc + 1], ALU.mult, [stok, "fpn"], [wtok])
                    self.ts("pool", wb[:, 384:513], sg[:, 384:513], kvn[:, kc:kc + 1], ALU.mult, [stok, "kvn"], [wtok])
                    self.dma(self.W2s[h, :, kc * 513:(kc + 1) * 513], wb[:, 0:513], f"w2s{slot}", [wtok], [f"W2s{h}"], eng="pool")
                self.prep_add(fin, frest)
        for hh in range(8):
            def fin(slot, hh=hh):
                self.dma(stg[slot][:, 0:1024], wout2_d[:, hh, :], f"stg{slot}", [], [f"stg{slot}"], eng="pool")

            def frest(slot, hh=hh):
                sg, wb = stg[slot], wst[slot]
                self.cp("pool", wb[:, 0:1024], sg[:, 0:1024], [f"stg{slot}"], [f"wst{slot}"])
                self.dma(self.Wout2s[:, hh * 1024:(hh + 1) * 1024], wb[:, 0:1024], f"w2s{slot}", [f"wst{slot}"], ["Wout2s"], eng="pool")
            self.prep_add(fin, frest)

    def layer2(self, h1_d, h1_tok, out_d):
        P = self.P
        L, TG, NG, NB = self.L, self.TG, self.NG, self.NB
        cst = self.cst
        PS = self.PS
        og2s = self.dscr("og2s", [NB, 128, 8, 128], BF16)
        cs = self.dscr("cs", [8, NB, 128], F32)
        self.arena_reset()
        ht = [self.al(f"ht{i}", [128, 1024], F32) for i in range(2)]
        junk = self.al("junk2", [128, 1024], BF16)
        sm = self.al("sm2", [128, 16], F32)
        phc_mark = self.a_off
        hnT = self.al("hnT", [128, 8, L], BF16)
        W2h = [self.al(f"W2h{i}", [128, 8, 513], BF16) for i in range(2)]
        KT = self.al("KT", [128, L], BF16)
        QT = self.al("QT", [128, L], BF16)
        ZT = self.al("ZT", [128, L], BF16)
        V = self.al("V", [128, NB, 128], BF16)
        hn2 = [self.al(f"hn{i}", [128, 1024], BF16) for i in range(2)]
        gk = self.al("gk", [128, 1], F32)
        gq = self.al("gq", [128, 1], F32)
        nfb = self.al("nfb", [128, 8], F32)
        sqb = [self.al(f"sqb{i}", [128, 512], BF16) for i in range(2)]
        lnt = [self.al(f"lnt{i}", [128, 512], F32) for i in range(2)]
        fcol = self.al("fcol", [128, NB], F32)
        lf = self.al("lf", [128, NB], F32)
        pre = [self.al(f"pre{i}", [128, NB], F32) for i in range(2)]
        ctok = self.al("ctok", [128, NB], F32)
        negc = self.al("negc", [128, NB], F32)
        cT = self.al("cT", [NB, 128], F32)
        cqb = [self.al(f"cqb{i}", [128, 512], F32) for i in range(2)]
        cqm = [self.al(f"cqm{i}", [128, 4, 128], F32) for i in range(2)]
        tt_ = [self.al(f"tt{i}", [128, 512], F32) for i in range(4)]
        pp_ = [self.al(f"pp{i}", [128, 512], BF16) for i in range(4)]
        SBK = [0, 1, 2, 7]
        rl = self.al("rl", [128, 512], F32)
        o1 = self.al("o1", [128, 512], F32)
        ogt = [self.al(f"ogt{i}", [128, 512], BF16) for i in range(2)]

        self.dma(gk[:], self.gk_d, "iniB", [], ["gk"])
        self.dma(gq[:], self.gq_d, "iniB", [], ["gq"])
        self.dma(nfb[:], self.fb_d, "iniB", [], ["nfb"])
        self.join(["gk", "gq", "nfb"], 11)
        self.ts("dve", gq[:], gq[:], 128.0 ** -0.5, ALU.mult, ["gq"], ["gq"])
        self.ts("dve", nfb[:], nfb[:], -1.0, ALU.mult, ["nfb"], ["nfb"])

        def a1(b):
            hs = ht[b % 2]
            htok = f"ht{b % 2}"
            so_ = 3 * (b % 2)
            self.dma(hs[:], h1_d[b * 128:(b + 1) * 128, :], htok, [h1_tok], [htok])
            self.act(junk[:], hs[:], AF.Square, [htok], ["junk2", f"s2a{so_}"], accum_out=sm[:, so_:so_ + 1])
            self.act(sm[:, so_ + 1:so_ + 2], sm[:, so_:so_ + 1], AF.Ln, [f"s2a{so_}"], [f"s2a{so_ + 1}"], scale=1.0 / D, bias=EPS)
            self.act(sm[:, so_ + 2:so_ + 3], sm[:, so_ + 1:so_ + 2], AF.Exp, [f"s2a{so_ + 1}"], [f"s2a{so_ + 2}"], scale=-0.5)
            self.ts("dve", hn2[b % 2][:], hs[:], sm[:, so_ + 2:so_ + 3], ALU.mult, [htok, f"s2a{so_ + 2}"], [f"hn{b % 2}"])

        def a2(b):
            pT = self.bank_bf(b % 2).rearrange("p (k i) -> p k i", k=8)
            for kc in range(8):
                self.tr(pT[:, kc, :], hn2[b % 2][:, kc * 128:(kc + 1) * 128], [f"hn{b % 2}"], [f"ps{b % 2}"])
            self.cp("act" if b % 2 == 0 else "dve", hnT[:, :, b * 128:(b + 1) * 128], pT, [f"ps{b % 2}"], ["hnT"])

        a1(0)
        for b in range(NB):
            if b + 1 < NB:
                a1(b + 1)
            a2(b)

        nproj = 0
        nsc = 0
        for h in range(8):
            Wh = W2h[h % 2]
            wtok = f"W2h{h % 2}"
            self.dma(Wh[:].rearrange("p k n -> p (k n)"), self.W2s[h], wtok, [f"W2s{h}"], [wtok])
            for b in range(NB):
                vb = 4 + (b % 2)
                for kc in range(8):
                    self.mm(self.bank(vb)[:, 0:129], hnT[:, kc, b * 128:(b + 1) * 128], Wh[:, kc, 384:513], kc == 0, kc == 7,
                            [wtok, "hnT"], [f"ps{vb}"])
                self.cp("act", V[:, b, :], self.bank(vb)[:, 0:128], [f"ps{vb}"], ["V"])
                self.cp("dve", fcol[:, b:b + 1], self.bank(vb)[:, 128:129], [f"ps{vb}"], ["fcol"])
            self.act(lf[:], fcol[:], AF.Exp, ["fcol", "nfb"], ["lf"], scale=-1.0, bias=nfb[:, h:h + 1])
            self.act(lf[:], lf[:], AF.Ln, ["lf"], ["lf"], bias=1.0)
            self.ts("dve", lf[:], lf[:], -1.0, ALU.mult, ["lf"], ["lf"])
            o6 = 6 * 512
            self.mm(PS[:, o6:o6 + NB], cst[:, C_LFULL, :], lf[:], True, True, ["lf", "cst"], ["ps6"])
            self.mm(PS[:, o6 + 64:o6 + 64 + NB], cst[:, C_ONES, :], lf[:], True, True, ["lf", "cst"], ["ps6"])
            self.cp("dve", pre[0][:], PS[:, o6 + 64:o6 + 64 + NB], ["ps6"], ["pre0"])
            cur = 0
            st_ = 1
            while st_ < NB:
                nxt = 1 - cur
                self.cp("dve", pre[nxt][:, 0:st_], pre[cur][:, 0:st_], [f"pre{cur}"], [f"pre{nxt}"])
                self.tt("dve", pre[nxt][:, st_:NB], pre[cur][:, st_:NB], pre[cur][:, 0:NB - st_], ALU.add, [f"pre{cur}"], [f"pre{nxt}"])
                cur = nxt
                st_ *= 2
            self.tt("dve", ctok[:], pre[cur][:], PS[:, o6 + 64:o6 + 64 + NB], ALU.subtract, [f"pre{cur}", "ps6"], ["ctok"])
            self.tt("dve", ctok[:], ctok[:], PS[:, o6:o6 + NB], ALU.add, ["ctok", "ps6"], ["ctok"])
            self.ts("dve", negc[:], ctok[:], -1.0, ALU.mult, ["ctok"], ["negc"])
            self.mm(PS[0:NB, o6 + 128:o6 + 256], ctok[:], cst[:, C_ID, :], True, True, ["ctok", "cst"], ["ps6"])
            self.cp("act", cT[:], PS[0:NB, o6 + 128:o6 + 256], ["ps6"], ["cT"])
            self.dma(cs[h], cT[:], "cs", ["cT"], [f"cs{h}"])
            for which, (c0, dst, gain) in enumerate(((0, KT, gk), (128, QT, gq))):
                for tg in range(NG):
                    cols = slice(tg * TG, (tg + 1) * TG)
                    pb = nproj % 2
                    nproj += 1
                    for kc in range(8):
                        self.mm(self.bank(pb)[:, 0:TG], Wh[:, kc, c0:c0 + 128], hnT[:, kc, cols], kc == 0, kc == 7, [wtok, "hnT"], [f"ps{pb}"])
                    sb_ = sqb[pb]
                    self.act(sb_[:, 0:TG], self.bank(pb)[:, 0:TG], AF.Square, [f"ps{pb}"], [f"sqb{pb}"])
                    self.mm(self.bank(2 + pb)[:, 0:TG], self.onesb[:], sb_[:, 0:TG], True, True, [f"sqb{pb}", "onesb"], [f"ps{2 + pb}"])
                    ln_ = lnt[pb]
                    self.act(ln_[:, 0:TG], self.bank(2 + pb)[:, 0:TG], AF.Ln, [f"ps{2 + pb}"], [f"lnt{pb}"], scale=1.0 / 128, bias=EPS)
                    self.act(ln_[:, 0:TG], ln_[:, 0:TG], AF.Exp, [f"lnt{pb}"], [f"lnt{pb}"], scale=-0.5)
                    self.stt("dve", dst[:, cols], self.bank(pb)[:, 0:TG], gain[:, 0:1], ln_[:, 0:TG], ALU.mult, ALU.mult,
                             [f"ps{pb}", f"lnt{pb}", "gk", "gq"], ["KT" if which == 0 else "QT"])
            for tg in range(NG):
                cols = slice(tg * TG, (tg + 1) * TG)
                pb = nproj % 2
                nproj += 1
                for kc in range(8):
                    self.mm(self.bank(pb)[:, 0:TG], Wh[:, kc, 256:384], hnT[:, kc, cols], kc == 0, kc == 7, [wtok, "hnT"], [f"ps{pb}"])
                self.act(ZT[:, cols], self.bank(pb)[:, 0:TG], AF.Silu, [f"ps{pb}"], ["ZT"])
            nqb = TG // 128
            tiles = [(jq, kb) for jq in range(NG) for kb in range(nqb * jq + nqb)]
            LOOK = 3

            def pre_q(jq, h=h):
                q0 = jq * TG
                cq = cqb[jq % 2]
                ctk = f"cqb{jq % 2}"
                self.dma(cq[:, 0:TG], cs[h].rearrange("b i -> (b i)")[q0:q0 + TG].rearrange("(o n) -> o n", o=1).partition_broadcast(128),
                         ctk, [f"cs{h}"], [ctk])
                for d_ in range(nqb):
                    self.tt("pool", cqm[jq % 2][:, d_, :], cq[:, d_ * 128:(d_ + 1) * 128], cst[:, C_MTRI, :], ALU.add,
                            [ctk, "cst"], [f"cqm{jq % 2}_{d_}"])

            def s_mm(n):
                jq, kb = tiles[n]
                q0 = jq * TG
                d_ = kb - nqb * jq
                c0 = 0 if d_ < 0 else d_ * 128
                si = n % 4
                sbk = SBK[si]
                self.mm(self.bank(sbk)[:, c0:TG], KT[:, kb * 128:(kb + 1) * 128], QT[:, q0 + c0:q0 + TG], True, True,
                        ["KT", "QT"], [f"ps{sbk}"])

            def rest(n, h=h):
                jq, kb = tiles[n]
                q0 = jq * TG
                d_ = kb - nqb * jq
                c0 = 0 if d_ < 0 else d_ * 128
                si = n % 4
                sbk = SBK[si]
                tt = tt_[si]
                pp = pp_[si]
                cq = cqb[jq % 2]
                ctk = f"cqb{jq % 2}"
                ob = 3 + (jq % 2)
                lb = 5 + (jq % 2)
                nk_ = nqb * jq + nqb
                if d_ >= 0:
                    self.tt("dve", tt[:, c0:c0 + 128], self.bank(sbk)[:, c0:c0 + 128], cqm[jq % 2][:, d_, :], ALU.add,
                            [f"ps{sbk}", f"cqm{jq % 2}_{d_}"], [f"tt{si}"])
                    if c0 + 128 < TG:
                        self.tt("dve", tt[:, c0 + 128:TG], self.bank(sbk)[:, c0 + 128:TG], cq[:, c0 + 128:TG], ALU.add,
                                [f"ps{sbk}", ctk], [f"tt{si}"])
                else:
                    self.tt("dve", tt[:, 0:TG], self.bank(sbk)[:, 0:TG], cq[:, 0:TG], ALU.add, [f"ps{sbk}", ctk], [f"tt{si}"])
                self.act(pp[:, c0:TG], tt[:, c0:TG], AF.Exp, [f"tt{si}", "negc"], [f"pp{si}"], bias=negc[:, kb:kb + 1])
                self.mm(self.bank(ob)[:, c0:TG], V[:, kb, :], pp[:, c0:TG], kb == 0, kb == nk_ - 1, ["V", f"pp{si}"], [f"ps{ob}"])
                self.mm(self.bank(lb)[:, c0:TG], self.onesb[:], pp[:, c0:TG], kb == 0, kb == nk_ - 1, ["onesb", f"pp{si}"], [f"ps{lb}"])

            def post(jq, h=h):
                q0 = jq * TG
                ob = 3 + (jq % 2)
                lb = 5 + (jq % 2)
                self.act(rl[:, 0:TG], self.bank(lb)[:, 0:TG], AF.Ln, [f"ps{lb}"], ["rl"])
                self.act(rl[:, 0:TG], rl[:, 0:TG], AF.Exp, ["rl"], ["rl"], scale=-1.0)
                self.tt("dve", o1[:, 0:TG], self.bank(ob)[:, 0:TG], rl[:, 0:TG], ALU.mult, [f"ps{ob}", "rl"], ["o1"])
                og_ = ogt[jq % 2]
                self.tt("pool", og_[:, 0:TG], o1[:, 0:TG], ZT[:, q0:q0 + TG], ALU.mult, ["o1", "ZT"], [f"ogt{jq % 2}"])
                self.dma(og2s[q0 // 128:(q0 + TG) // 128, :, h, :].rearrange("b p t -> p b t"),
                         og_[:, 0:TG].rearrange("p (b t) -> p b t", t=128), f"ogst{jq % 2}", [f"ogt{jq % 2}"], ["og2s"], eng="pool")

            pre_q(0)
            if NG > 1:
                pre_q(1)
            for n in range(len(tiles) + LOOK + 2):
                if n < len(tiles):
                    s_mm(n)
                m = n - LOOK
                if m >= 0 and m < len(tiles):
                    rest(m)
                m2 = m - 2
                if m2 >= 0 and m2 < len(tiles):
                    jq_m, kb_m = tiles[m2]
                    if kb_m == nqb * jq_m + nqb - 1:
                        post(jq_m)
                        if jq_m + 2 < NG:
                            pre_q(jq_m + 2)

        self.barrier("l2c")
        self.a_off = phc_mark
        Wout2 = self.al("Wout2", [128, 8, 1024], BF16)
        ogb = [self.al(f"ogb{i}", [128, 8, 128], BF16) for i in range(3)]
        yts = [self.al(f"yt2_{i}", [128, 1024], F32) for i in range(2)]
        smc = self.al("smc", [128, 16], F32)
        gpost = self.al("gpost2", [128, 1024], F32)
        self.dma(gpost[:], self.gpost2_d.partition_broadcast(128), "iniC", [], ["gpost2"])
        for hh in range(8):
            self.dma(Wout2[:, hh, :], self.Wout2s[:, hh * 1024:(hh + 1) * 1024], "iniC", ["Wout2s"], [f"Wout2_{hh}"],
                     eng="sp")
        self.join(["gpost2"] + [f"Wout2_{q}" for q in range(8)], 12)
        for b in range(NB):
            ob_ = ogb[b % 3]
            otok = f"ogb{b % 3}"
            self.dma(ob_[:], og2s[b], otok, ["og2s"], [otok])
            bp = 2 * (b % 4)
            for half in range(2):
                for hh in range(8):
                    self.mm(self.bank(bp + half), ob_[:, hh, :], Wout2[:, hh, half * 512:(half + 1) * 512], hh == 0, hh == 7,
                            [otok, f"Wout2_{hh}"], [f"ps{bp + half}"])
            py = self.bank(bp, 2)
            so_ = 4 * (b % 4)
            self.act(junk[:], py, AF.Square, [f"ps{bp}", f"ps{bp + 1}"], ["junk2", f"s2c{so_}"], accum_out=smc[:, so_:so_ + 1])
            self.act(smc[:, so_ + 1:so_ + 2], smc[:, so_:so_ + 1], AF.Ln, [f"s2c{so_}"], [f"s2c{so_ + 1}"], scale=1.0 / D, bias=EPS)
            self.act(smc[:, so_ + 2:so_ + 3], smc[:, so_ + 1:so_ + 2], AF.Exp, [f"s2c{so_ + 1}"], [f"s2c{so_ + 2}"], scale=-0.5)
            yt = yts[b % 2]
            self.stt("dve", yt[:], py, smc[:, so_ + 2:so_ + 3], gpost[:], ALU.mult, ALU.mult,
                     [f"ps{bp}", f"ps{bp + 1}", f"s2c{so_ + 2}", "gpost2"], [f"yt2_{b % 2}"])
            hs = ht[b % 2]
            htok = f"ht{b % 2}"
            self.dma(hs[:], h1_d[b * 128:(b + 1) * 128, :], htok, [h1_tok], [htok])
            self.tt("pool", hs[:], yt[:], hs[:], ALU.add, [f"yt2_{b % 2}", htok], [htok])
            self.dma(out_d[b * 128:(b + 1) * 128, :], hs[:], f"ost{b % 2}", [htok], ["out_dram"], eng="pool")
        return "out_dram"

    def build(self):
        L = self.L
        self.setup_common()
        if self.mode == "l1":
            x_d = self.din("x", [L, D])
            h1_d = self.dout("h1", [L, D])
            tok = self.layer1(x_d, h1_d)
            self.final_tokens.append(tok)
        elif self.mode == "l2":
            h1_d = self.din("h1", [L, D])
            out_d = self.dout("out", [L, D])
            self.layer2_prep()
            self.prep_emit(len(self.prep))
            tok = self.layer2(h1_d, "h1_in", out_d)
            self.final_tokens.append(tok)
        else:
            x_d = self.din("x", [L, D])
            h1_d = self.dscr("h1", [L, D])
            out_d = self.dout("out", [L, D])
            tok1 = self.layer1(x_d, h1_d)
            self.prep_emit(len(self.prep))
            self.barrier("l12")
            tok = self.layer2(h1_d, tok1, out_d)
            self.final_tokens.append(tok)
        self.P.finalize(final_wait_tokens=self.final_tokens)
        return self.nc


def prep_shared_inputs(inp):
    f = lambda a: np.ascontiguousarray(a, dtype=np.float32)
    w_in = inp["gdn_w_in"][0]
    wq = w_in[:, :3072].reshape(8, 128, 24, 128).transpose(2, 1, 0, 3)
    wz = w_in[:, 3072:4112].reshape(8, 128, 1040).transpose(1, 0, 2)
    sh = {
        "consts": make_consts(),
        "w1qkv": f(wq),
        "w1z": f(wz),
        "gpre": f(inp["gdn_pre_norm"][0].reshape(8, 128).T),
        "cw": f(inp["gdn_conv_w"][0].reshape(4, 24, 128).transpose(2, 1, 0)),
        "alog_b": f(np.broadcast_to(inp["gdn_a_log"][0][None, :], (128, 8))),
        "dtb_b": f(np.broadcast_to(inp["gdn_dt_bias"][0][None, :], (128, 8))),
        "gonorm": f(inp["gdn_o_norm"][0].reshape(128, 1)),
        "wout1": f(inp["gdn_w_out"][0].reshape(8, 128, 1024).transpose(1, 0, 2)),
        "gpost1": f(inp["gdn_post_norm"][0].reshape(1, 1024)),
    }
    kvw = inp["kv_w"]
    fw = inp["fox_w_in"][0]
    w2 = np.zeros((8, 1024, 513), np.float32)
    for h in range(8):
        w2[h, :, 0:128] = kvw[:, h * 128:(h + 1) * 128]
        w2[h, :, 128:256] = fw[:, h * 128:(h + 1) * 128]
        w2[h, :, 256:384] = fw[:, 1024 + h * 128:1024 + (h + 1) * 128]
        w2[h, :, 384:512] = kvw[:, 1024 + h * 128:1024 + (h + 1) * 128]
        w2[h, :, 512] = kvw[:, 2048 + h]
    sh.update({
        "w2": f(w2.reshape(8, 8, 128, 513).transpose(0, 2, 1, 3)),
        "kvn": f(inp["kv_norm"].reshape(8, 128).T),
        "fpn": f(inp["fox_pre_norm"][0].reshape(8, 128).T),
        "wout2": f(inp["fox_w_out"][0].reshape(8, 128, 1024).transpose(1, 0, 2)),
        "gk": f(inp["kv_k_norm"].reshape(128, 1)),
        "gq": f(inp["fox_q_norm"][0].reshape(128, 1)),
        "fb_b": f(np.broadcast_to(inp["kv_forget_bias"][None, :], (128, 8))),
        "gpost2": f(inp["fox_post_norm"][0].reshape(1, 1024)),
    })
    return sh


FUSED = True
SEQ = 4096
NCORES = 8


def _run(mode, per_core_extra, sh, out_name):
    b = Builder(SEQ, mode=mode)
    nc = b.build()
    base = {k: v for k, v in sh.items() if k in b.dram}
    in_maps = []
    for c in range(NCORES):
        m = dict(base)
        m.update(per_core_extra[c])
        in_maps.append(m)
    res = run_bass_kernel_spmd(nc, in_maps, core_ids=list(range(NCORES)))
    return [np.asarray(r[out_name]) for r in res.results]


def kernel(**inputs):
    inp = {k: np.asarray(v) for k, v in inputs.items()}
    sh = prep_shared_inputs(inp)
    x = np.ascontiguousarray(inp["x"], dtype=np.float32)
    if FUSED:
        outs = _run("both", [{"x": x[c]} for c in range(NCORES)], sh, "out")
    else:
        h1 = _run("l1", [{"x": x[c]} for c in range(NCORES)], sh, "h1")
        outs = _run("l2", [{"h1": np.ascontiguousarray(h1[c])} for c in range(NCORES)], sh, "out")
    return np.stack(outs, axis=0).astype(np.float32)
```
